# Optimizing a Trainium2 kernel written in Bass

```python
import jax
import jax.numpy as jnp
from jax import lax
import numpy as np

D_MODEL = 2048
BATCH = 4
SEQ = 8192
DEPTH = 4
DEC_BATCH = 8
DEC_SEQ = 32
PAST_LEN = 2048

CHUNK = 64
D_MIX = D_MODEL
EPS = 1e-6
NEG_INF = -1e30
D_CONV = D_MIX // 4
CONV_W = 31
CONV_GROUPS = 8
D_ATT = D_MIX // 4
N_HEADS = 8
HEAD_DIM = D_ATT // N_HEADS
BAND_CHUNKS = 8
BAND_PAST = BAND_CHUNKS * CHUNK
BAND_LEN = (BAND_CHUNKS + 1) * CHUNK
REL_CLIP = 128
D_POOL = D_MIX // 4
POOL_WINDOWS = (2, 4, 8, 16)
POOL_GROUP = D_POOL // 4
POOL_HIST = 15
D_SG = D_MIX // 4
SG_CHUNK = 128
SG_GROUPS = 4
SG_GROUP = D_SG // SG_GROUPS
D_FF = -(-8 * D_MODEL // (3 * 256)) * 256
D_IN = 2 * D_CONV + 3 * D_ATT + D_POOL + 2 * D_SG
IN_SPLITS = (D_CONV, 2 * D_CONV, 2 * D_CONV + D_ATT, 2 * D_CONV + 2 * D_ATT,
             2 * D_CONV + 3 * D_ATT, 2 * D_CONV + 3 * D_ATT + D_POOL,
             2 * D_CONV + 3 * D_ATT + D_POOL + D_SG)

kernel_name = 'hybrid_streaming_encoder_step'


def rms_norm(x, g):
    xf = x.astype(jnp.float32)
    y = xf * lax.rsqrt(jnp.mean(xf * xf, axis=-1, keepdims=True) + EPS)
    return (y * g).astype(x.dtype)


def group_norm(x, g, b, groups):
    shp = x.shape
    xf = x.astype(jnp.float32).reshape(shp[:-1] + (groups, shp[-1] // groups))
    mu = jnp.mean(xf, axis=-1, keepdims=True)
    var = jnp.mean(jnp.square(xf - mu), axis=-1, keepdims=True)
    y = ((xf - mu) * lax.rsqrt(var + EPS)).reshape(shp)
    return (y * g + b).astype(x.dtype)


def conv_module(za, zg, hist, conv_w, conv_b, gn_g, gn_b, conv_pw):
    u = za * jax.nn.sigmoid(zg)
    ext = jnp.concatenate([hist.astype(u.dtype), u], axis=1)
    y = lax.conv_general_dilated(ext, conv_w[:, None, :], window_strides=(1,), padding='VALID',
                                 dimension_numbers=('NWC', 'WIO', 'NWC'),
                                 feature_group_count=D_CONV) + conv_b
    y = jax.nn.silu(group_norm(y, gn_g, gn_b, CONV_GROUPS)) @ conv_pw
    return y, ext[:, -(CONV_W - 1):]


def band_attn_prompt(q, k, v, rel_bias):
    B, T, H, Dh = q.shape
    nc = T // CHUNK
    qc = q.reshape(B, nc, CHUNK, H, Dh)
    pad = jnp.zeros((B, BAND_PAST, H, Dh), k.dtype)
    kp = jnp.concatenate([pad, k], axis=1).reshape(B, nc + BAND_CHUNKS, CHUNK, H, Dh)
    vp = jnp.concatenate([pad, v], axis=1).reshape(B, nc + BAND_CHUNKS, CHUNK, H, Dh)
    idx = jnp.arange(nc)[:, None] + jnp.arange(BAND_CHUNKS + 1)[None, :]
    kb = kp[:, idx].reshape(B, nc, BAND_LEN, H, Dh)
    vb = vp[:, idx].reshape(B, nc, BAND_LEN, H, Dh)
    j = jnp.arange(BAND_LEN)
    i = jnp.arange(CHUNK)
    rel = jnp.clip(j[None, :] - BAND_PAST - i[:, None], -REL_CLIP, REL_CLIP) + REL_CLIP
    bias = rel_bias[:, rel].astype(jnp.float32)
    valid = j[None, :] >= (BAND_CHUNKS - jnp.arange(nc))[:, None] * CHUNK
    s = jnp.einsum('bnqhd,bnkhd->bnhqk', qc, kb, preferred_element_type=jnp.float32) * (Dh ** -0.5) + bias
    s = jnp.where(valid[None, :, None, None, :], s, NEG_INF)
    p = jax.nn.softmax(s, axis=-1).astype(v.dtype)
    o = jnp.einsum('bnhqk,bnkhd->bnqhd', p, vb)
    return o.reshape(B, T, H * Dh)


def band_attn_sample(q, k, v, k_cache, v_cache, rel_bias):
    B, T, H, Dh = q.shape
    L = k_cache.shape[1]
    kk = jnp.concatenate([k_cache.astype(k.dtype), k], axis=1)
    vv = jnp.concatenate([v_cache.astype(v.dtype), v], axis=1)
    kpos = jnp.arange(L + T) - L
    rel = jnp.clip(kpos[None, :] - jnp.arange(T)[:, None], -REL_CLIP, REL_CLIP) + REL_CLIP
    bias = rel_bias[:, rel].astype(jnp.float32)
    s = jnp.einsum('bqhd,bkhd->bhqk', q, kk, preferred_element_type=jnp.float32) * (Dh ** -0.5) + bias
    p = jax.nn.softmax(s, axis=-1).astype(v.dtype)
    o = jnp.einsum('bhqk,bkhd->bqhd', p, vv)
    return o.reshape(B, T, H * Dh)


def pool_mixer(p, hist, pos0, pool_w, pool_scale):
    B, T, C = p.shape
    ext = jnp.concatenate([hist.astype(p.dtype), p], axis=1)
    extf = ext.astype(jnp.float32)
    cs = jnp.concatenate([jnp.zeros((B, 1, C), jnp.float32), lax.cumsum(extf, axis=1)], axis=1)
    pos = pos0 + jnp.arange(T)
    outs = []
    for gi, w in enumerate(POOL_WINDOWS):
        sl = slice(gi * POOL_GROUP, (gi + 1) * POOL_GROUP)
        win = cs[:, POOL_HIST + 1:POOL_HIST + 1 + T, sl] - cs[:, POOL_HIST + 1 - w:POOL_HIST + 1 - w + T, sl]
        cnt = jnp.minimum(w, pos + 1).astype(jnp.float32)[None, :, None]
        outs.append(win / cnt - extf[:, POOL_HIST:, sl])
    m = jnp.concatenate(outs, axis=-1).reshape(B, T, len(POOL_WINDOWS), POOL_GROUP).astype(p.dtype)
    y = jnp.einsum('btgc,gcd->btgd', m, pool_w).reshape(B, T, C) * pool_scale
    return y, ext[:, -POOL_HIST:]


def spatial_gating(u, v, ln_g, ln_b, sg_w, sg_b):
    B, T, C = v.shape
    vn = group_norm(v, ln_g, ln_b, 1)
    L = min(T, SG_CHUNK)
    n = T // L
    vc = vn.reshape(B, n, L, SG_GROUPS, SG_GROUP)
    w = sg_w[:, :L, :L] * jnp.tril(jnp.ones((L, L), sg_w.dtype))
    s = jnp.einsum('gij,bnjgc->bnigc', w, vc) + sg_b[:, :L].T[None, None, :, :, None]
    return u * s.reshape(B, T, C), vn


def trunk_layer(x, c, lp, conv_hist, pool_hist, k_cache, v_cache, pos0):
    (ada_w, ada_b, n1, n2, w_in, conv_w, conv_b, gn_g, gn_b, conv_pw, rel_bias,
     pool_w, pool_scale, sg_g, sg_bn, sg_w, sg_b, w_out, wg, wu, wd) = lp
    B, T, _ = x.shape
    mod = (jax.nn.silu(c) @ ada_w + ada_b)[:, None, :]
    sh1, sc1, g1, sh2, sc2, g2 = jnp.split(mod, 6, axis=-1)
    h = rms_norm(x, n1) * (1 + sc1) + sh1
    z = h @ w_in
    za, zg, zq, zk, zv, zp, zu, zs = jnp.split(z, IN_SPLITS, axis=-1)
    if conv_hist is None:
        conv_hist = jnp.zeros((B, CONV_W - 1, D_CONV), x.dtype)
        pool_hist = jnp.zeros((B, POOL_HIST, D_POOL), x.dtype)
    ya, conv_state = conv_module(za, zg, conv_hist, conv_w, conv_b, gn_g, gn_b, conv_pw)
    q = zq.reshape(B, T, N_HEADS, HEAD_DIM)
    k = zk.reshape(B, T, N_HEADS, HEAD_DIM)
    v = zv.reshape(B, T, N_HEADS, HEAD_DIM)
    if k_cache is None:
        yb = band_attn_prompt(q, k, v, rel_bias)
        k_rows, v_rows = k[:, -BAND_PAST:], v[:, -BAND_PAST:]
    else:
        yb = band_attn_sample(q, k, v, k_cache, v_cache, rel_bias)
        k_rows, v_rows = k, v
    yc, pool_state = pool_mixer(zp, pool_hist, pos0, pool_w, pool_scale)
    yd, sg_v = spatial_gating(zu, zs, sg_g, sg_bn, sg_w, sg_b)
    mix = jnp.concatenate([ya, yb, yc, yd], axis=-1) @ w_out
    x = x + g1 * mix
    h2 = rms_norm(x, n2) * (1 + sc2) + sh2
    x = x + g2 * ((jax.nn.silu(h2 @ wg) * (h2 @ wu)) @ wd)
    return x, (conv_state, k_rows, v_rows, pool_state, sg_v)


def setup_inputs(seed: int = 0) -> dict:
    key = jax.random.key(seed)
    ks = iter(jax.random.split(key, 40))

    def nrm(shape, s):
        return jax.random.normal(next(ks), shape, jnp.float32) * s

    kv_len = min(BAND_PAST, PAST_LEN)
    return {
        'x_prompt': nrm((BATCH, SEQ, D_MODEL), 1.0),
        'x_sample': nrm((DEC_BATCH, DEC_SEQ, D_MODEL), 1.0),
        'c_prompt': nrm((BATCH, D_MODEL), 1.0),
        'c_sample': nrm((DEC_BATCH, D_MODEL), 1.0),
        'cache_conv': nrm((DEPTH, DEC_BATCH, CONV_W - 1, D_CONV), 0.5),
        'cache_k': nrm((DEPTH, DEC_BATCH, kv_len, N_HEADS, HEAD_DIM), 1.0),
        'cache_v': nrm((DEPTH, DEC_BATCH, kv_len, N_HEADS, HEAD_DIM), 1.0),
        'cache_pool': nrm((DEPTH, DEC_BATCH, POOL_HIST, D_POOL), 1.0),
        'ada_w': nrm((DEPTH, D_MODEL, 6 * D_MODEL), 0.5 * D_MODEL ** -0.5),
        'ada_b': nrm((DEPTH, 6 * D_MODEL), 0.02),
        'norm1_g': 1.0 + nrm((DEPTH, D_MODEL), 0.02),
        'norm2_g': 1.0 + nrm((DEPTH, D_MODEL), 0.02),
        'w_in': nrm((DEPTH, D_MODEL, D_IN), D_MODEL ** -0.5),
        'conv_w': nrm((DEPTH, CONV_W, D_CONV), CONV_W ** -0.5),
        'conv_b': nrm((DEPTH, D_CONV), 0.02),
        'conv_gn_g': 1.0 + nrm((DEPTH, D_CONV), 0.02),
        'conv_gn_b': nrm((DEPTH, D_CONV), 0.02),
        'conv_pw': nrm((DEPTH, D_CONV, D_CONV), D_CONV ** -0.5),
        'rel_bias': nrm((DEPTH, N_HEADS, 2 * REL_CLIP + 1), 0.5),
        'pool_w': nrm((DEPTH, len(POOL_WINDOWS), POOL_GROUP, POOL_GROUP), POOL_GROUP ** -0.5),
        'pool_scale': 1.0 + nrm((DEPTH, D_POOL), 0.1),
        'sg_ln_g': 1.0 + nrm((DEPTH, D_SG), 0.02),
        'sg_ln_b': nrm((DEPTH, D_SG), 0.02),
        'sg_w': nrm((DEPTH, SG_GROUPS, SG_CHUNK, SG_CHUNK), 0.5 * SG_CHUNK ** -0.5),
        'sg_b': 1.0 + nrm((DEPTH, SG_GROUPS, SG_CHUNK), 0.1),
        'w_out': nrm((DEPTH, D_MIX, D_MODEL), D_MIX ** -0.5),
        'ffn_gate': nrm((DEPTH, D_MODEL, D_FF), D_MODEL ** -0.5),
        'ffn_up': nrm((DEPTH, D_MODEL, D_FF), D_MODEL ** -0.5),
        'ffn_down': nrm((DEPTH, D_FF, D_MODEL), D_FF ** -0.5),
        'final_g': 1.0 + nrm((D_MODEL,), 0.02),
    }


def reference(x_prompt, x_sample, c_prompt, c_sample, cache_conv, cache_k, cache_v, cache_pool,
              ada_w, ada_b, norm1_g, norm2_g, w_in, conv_w, conv_b, conv_gn_g, conv_gn_b, conv_pw,
              rel_bias, pool_w, pool_scale, sg_ln_g, sg_ln_b, sg_w, sg_b, w_out,
              ffn_gate, ffn_up, ffn_down, final_g):
    xp, xs = x_prompt, x_sample
    conv_p, conv_s, kp_l, vp_l, ks_l, vs_l, pool_p, pool_s, sgv_s = [], [], [], [], [], [], [], [], []
    for l in range(DEPTH):
        lp = (ada_w[l], ada_b[l], norm1_g[l], norm2_g[l], w_in[l], conv_w[l], conv_b[l],
              conv_gn_g[l], conv_gn_b[l], conv_pw[l], rel_bias[l], pool_w[l], pool_scale[l],
              sg_ln_g[l], sg_ln_b[l], sg_w[l], sg_b[l], w_out[l], ffn_gate[l], ffn_up[l], ffn_down[l])
        xp, (cp, kr, vr, pp, _) = trunk_layer(xp, c_prompt, lp, None, None, None, None, 0)
        xs, (cs_, ksr, vsr, ps, sgv) = trunk_layer(xs, c_sample, lp, cache_conv[l], cache_pool[l],
                                                   cache_k[l], cache_v[l], PAST_LEN)
        conv_p.append(cp)
        kp_l.append(kr)
        vp_l.append(vr)
        pool_p.append(pp)
        conv_s.append(cs_)
        ks_l.append(ksr)
        vs_l.append(vsr)
        pool_s.append(ps)
        sgv_s.append(sgv)
    y_prompt = rms_norm(xp, final_g)
    y_sample = rms_norm(xs, final_g)
    return (y_prompt, y_sample,
            jnp.stack(conv_p), jnp.stack(conv_s),
            jnp.stack(kp_l), jnp.stack(vp_l), jnp.stack(ks_l), jnp.stack(vs_l),
            jnp.stack(pool_p), jnp.stack(pool_s), jnp.stack(sgv_s))
```

```python
import os
import numpy as np
import concourse.bass as bass
import concourse.mybir as mybir
from concourse.bass_utils import run_bass_kernel_spmd
from contextlib import ExitStack

F32 = mybir.dt.float32
F32R = mybir.dt.float32r
BF16 = mybir.dt.bfloat16
AF = mybir.ActivationFunctionType
ALU = mybir.AluOpType

L = 4
D = 2048
NCH = 16
DIN = 4096
DFF = 5632
NFF = 44
TP = 512
TS = 32
EPS = 1e-6
RING = 6
SLOT = 4096


class Buf:
    __slots__ = ("lw", "rd", "name", "excl")

    def __init__(self, name="", excl=False):
        self.lw = []
        self.rd = {}
        self.name = name
        self.excl = excl


class Rec:
    __slots__ = ("eng", "fn", "deps", "need", "val", "sem", "dma", "prev")


class Sched:
    ENG = ("pe", "act", "dve", "pool", "sp")

    def __init__(self, nc, nch=6):
        self.nc = nc
        self.q = {e: [] for e in self.ENG}
        self.nch = nch
        self.chcnt = {("sp", i): 0 for i in range(nch)}
        self.chcnt.update({("pool", i): 0 for i in range(nch)})
        self.chrr = {"sp": 0, "pool": 0}
        self.ndma = 0

    def _track(self, r, reads, writes):
        deps = []
        ex = [b for b in reads if b.excl]
        if ex:
            reads = [b for b in reads if not b.excl]
            writes = list(writes) + ex
        for b in reads:
            deps.extend(b.lw)
        for b in writes:
            deps.extend(b.lw)
            deps.extend(b.rd.values())
        for b in reads:
            if r.dma:
                self.ndma += 1
                b.rd[("dma", self.ndma)] = r
            else:
                b.rd[r.eng] = r
        for b in writes:
            b.lw = [r]
            b.rd = {}
        ds = []
        seen = set()
        for d in deps:
            if d is r or id(d) in seen:
                continue
            seen.add(id(d))
            if d.eng == "pe" and r.eng == "pe" and not d.dma and not r.dma:
                continue
            d.need = True
            ds.append(d)
        r.deps = ds

    def op(self, eng, fn, reads=(), writes=()):
        r = Rec()
        r.eng = eng
        r.fn = fn
        r.need = False
        r.val = None
        r.sem = eng
        r.dma = False
        r.prev = 0
        self._track(r, reads, writes)
        self.q[eng].append(r)
        return r

    def dma(self, queue, out, in_, reads=(), writes=(), **kw):
        r = Rec()
        r.eng = queue
        r.dma = True
        ch = self.chrr[queue]
        self.chrr[queue] = (ch + 1) % self.nch
        key = (queue, ch)
        r.prev = 16 * self.chcnt[key]
        self.chcnt[key] += 1
        r.val = 16 * self.chcnt[key]
        r.sem = key
        r.need = True
        r.fn = lambda e: e.dma_start(out=out, in_=in_, **kw)
        self._track(r, reads, writes)
        self.q[queue].append(r)
        return r

    def emit(self, es):
        nc = self.nc
        sems = {}
        for e in self.ENG:
            sems[e] = es.enter_context(nc.semaphore("s_" + e))
        for k in self.chcnt:
            sems[k] = es.enter_context(nc.semaphore("d_%s%d" % k))
        for e in self.ENG:
            c = 0
            for r in self.q[e]:
                if r.dma:
                    continue
                if r.need:
                    c += 1
                    r.val = c
        block = es.enter_context(nc.Block())
        engs = {"pe": block.tensor, "act": block.scalar, "dve": block.vector,
                "pool": block.gpsimd, "sp": block.sync}
        finals = dict((k, 16 * v) for k, v in self.chcnt.items())
        for e in self.ENG:
            recs = self.q[e]

            def body(eng, recs=recs, e=e):
                seen = {}
                for r in recs:
                    if r.dma and r.prev > 0 and seen.get(r.sem, 0) < r.prev:
                        eng.wait_ge(sems[r.sem], r.prev)
                        seen[r.sem] = r.prev
                    for d in r.deps:
                        if seen.get(d.sem, 0) < d.val:
                            eng.wait_ge(sems[d.sem], d.val)
                            seen[d.sem] = d.val
                    ins = r.fn(eng)
                    if r.dma:
                        ins.then_inc(sems[r.sem], 16)
                    elif r.need:
                        ins.then_inc(sems[r.sem], 1)
                if e == "sp":
                    for k, v in finals.items():
                        if v > 0:
                            eng.wait_ge(sems[k], v)

            engs[e](body)


def build_program(NH, NR, NL, do_sample):
    STOP = int(os.environ.get('KSTOP', '99'))
    SUB = int(os.environ.get('KSUB', '99'))
    KB = int(os.environ.get('KB', '255'))
    NT = NH + NR
    nc = bass.Bass("TRN2", target_bir_lowering=False)
    S = Sched(nc)
    es = ExitStack()

    def din(name, shape, dt=F32):
        return nc.dram_tensor(name, list(shape), dt, kind="ExternalInput").ap()

    def dout(name, shape):
        return nc.dram_tensor(name, list(shape), F32, kind="ExternalOutput").ap()

    xp = din("xp", [NT * TP, D])
    flag_d = din("flag", [128, 1])
    xs = din("xs", [TS, D])
    cp = din("cp", [16, 128])
    cs = din("cs", [16, 128])
    cconv = din("cconv", [NL, 30, 512])
    ck = din("ck", [NL, 512, 512])
    cv = din("cv", [NL, 512, 512])
    cpool = din("cpool", [NL, 15, 512])
    ada_w = din("ada_w", [NL, D, 6 * D])
    ada_b = din("ada_b", [NL, 96, 128])
    n1 = din("n1", [NL, 16, 128])
    n2 = din("n2", [NL, 16, 128])
    w_in = din("w_in", [NL, D, DIN])
    conv_w = din("conv_w", [NL, 31, 512])
    conv_b = din("conv_b", [NL, 4, 128])
    gn_g = din("gn_g", [NL, 4, 128])
    gn_b = din("gn_b", [NL, 4, 128])
    conv_pw = din("conv_pw", [NL, 512, 512])
    rel_bias = din("rel_bias", [NL, 8, 257])
    pool_w = din("pool_w", [NL, 4, 128, 128])
    pool_scale = din("pool_scale", [NL, 4, 128])
    sg_g = din("sg_g", [NL, 512])
    sg_bn = din("sg_bn", [NL, 512])
    sg_w = din("sg_w", [NL, 4, 128, 128])
    sg_b = din("sg_b", [NL, 1, 512])
    w_out = din("w_out", [NL, D, D])
    wg = din("wg", [NL, D, DFF])
    wu = din("wu", [NL, D, DFF])
    wd = din("wd", [NL, DFF, D])
    final_g = din("final_g", [16, 128])

    yp = dout("yp", [max(NR, 1) * TP, D])
    ys = dout("ys", [TS, D])
    convp = dout("convp", [L, 30, 512])
    convs = dout("convs", [L, 30, 512])
    kp = dout("kp", [L, 512, 512])
    vp = dout("vp", [L, 512, 512])
    ks = dout("ks", [L, TS, 512])
    vs = dout("vs", [L, TS, 512])
    poolp = dout("poolp", [L, 15, 512])
    pools = dout("pools", [L, 15, 512])
    sgv = dout("sgv", [L, TS, 512])

    Gd = nc.dram_tensor("Gd", [L * 8 * 767], F32, kind="Internal").ap()
    NWC = 92 * NL
    WCL = [nc.dram_tensor("WC%d" % l_, [92, 128, SLOT], BF16, kind="Internal").ap() for l_ in range(NL)]
    Khist = nc.dram_tensor("Khist", [L, 128, 2048], BF16, kind="Internal").ap()
    Vhist = nc.dram_tensor("Vhist", [L, 128, 2048], BF16, kind="Internal").ap()
    bG = [Buf("G%d" % l) for l in range(L)]
    bKh = [Buf() for _ in range(L)]
    bVh = [Buf() for _ in range(L)]

    def sb(name, shape, dt):
        return es.enter_context(nc.sbuf_tensor(name, list(shape), dt))

    ident_f = sb("ident_f", [128, 128], F32)
    ident_b = sb("ident_b", [128, 128], BF16)
    ones_f = sb("ones_f", [128, 128], F32)
    ones_r = sb("ones_r", [128, 128], F32R)
    blk_b = sb("blk_b", [128, 128], BF16)
    ones_b = sb("ones_b", [128, 128], BF16)
    tril = sb("tril", [128, 128], F32)
    antiI = sb("antiI", [128, 128], F32)
    flag = sb("flagt", [128, 1], F32)
    invc = sb("invc", [128, 4, 16], F32)
    invd = sb("invd", [128, 4, 16], F32)
    AD = sb("AD", [128, L, 6, 16, 2], F32)
    modl = sb("modl", [128, 96, 2], F32)
    adabT = sb("adabT", [128, 96], F32)
    n1T = sb("n1T", [128, L, 16], F32)
    n2T = sb("n2T", [128, L, 16], F32)
    fgT = sb("fgT", [128, 16], F32)
    cbT = sb("cbT", [128, L, 4], F32)
    ggT = sb("ggT", [128, L, 4], F32)
    gbT = sb("gbT", [128, L, 4], F32)
    pscT = sb("pscT", [128, L, 4], F32)
    cwT = sb("cwT", [128, L, 4, 31], F32)
    WT = sb("WT", [128, L, 4, 128], BF16)
    b2f = sb("b2f", [1, 512], F32)
    b2r = sb("b2r", [1, 512], F32R)
    csf = sb("csf", [128, 16, 2], F32)
    csb = sb("csb", [128, 16, 2], BF16)
    hconv = sb("hconv", [128, L, 4, 30], BF16)
    hpool = sb("hpool", [128, L, 4, 15], F32)
    stg = sb("stg", [128, 2, 128], F32)
    st6 = sb("st6", [128, 6], F32)
    mv = sb("mv", [128, 2], F32)
    rv = sb("rv", [128, 2], F32)
    t16 = sb("t16", [128, 16], F32)
    u32 = sb("u32", [128, 4, 32], F32)
    ostage = sb("ostage", [128, 512], F32)

    b_const = Buf("const")
    b_par = Buf("par")
    b_stg = [Buf(), Buf()]
    b_hconv = [Buf() for _ in range(L)]
    b_hpool = [Buf() for _ in range(L)]
    b_ostage = Buf()
    b_u32 = Buf()
    b_small = Buf()
    b_b2 = Buf()
    b_ybf = Buf()

    xT = sb("xT", [128, 16, TP], F32)
    hT = sb("hT", [128, 16, TP], BF16)
    mixT = sb("mixT", [128, 16, TP], BF16)
    ring = sb("ring", [128, RING, SLOT], BF16)
    sq = sb("sq", [128, 2, TP], BF16)
    ybf = sb("ybf", [128, TP], BF16)
    rstd = sb("rstd", [128, TP], F32)
    sd = sb("sd", [128, TP], F32)
    tmp = sb("tmp", [128, 4, TP], F32)
    uext = sb("uext", [128, 4, 30 + TP], BF16)
    sl = sb("sl", [128, 2, TP], BF16)
    RA = sb("RA", [128, 2048], F32)
    RB = sb("RB", [128, 5120], BF16)
    RC = sb("RC", [128, 1280], F32)
    RD = sb("RD", [128, 4096], BF16)
    RE = sb("RE", [128, 4096], BF16)
    RF = sb("RF", [128, 2048], BF16)
    RG = sb("RG", [128, 3, 512], F32)

    bx = [Buf("x%d" % c) for c in range(16)]
    bh = [Buf("h%d" % c) for c in range(16)]
    bmix = [Buf("m%d" % c) for c in range(16)]
    bring = [Buf("ring%d" % i) for i in range(RING)]
    bsq = [Buf(), Buf()]
    brstd = Buf()
    bsd = Buf()
    btmp = [Buf(), Buf(), Buf(), Buf()]
    buext = [Buf() for _ in range(4)]
    bsl = [Buf(), Buf()]
    bRA = Buf("RA")
    bRB = [Buf() for _ in range(8)]
    bRC = [Buf(), Buf()]
    EXTB = [[bRB[0], bRB[1]], [bRB[1], bRB[2], bRB[3]], [bRB[3], bRB[4]], [bRB[4], bRB[5], bRB[6]]]
    PTMPB = [[bRC[0]], [bRC[0], bRC[1]]]
    bRD = [Buf() for _ in range(4)]
    bRE = [Buf() for _ in range(8)]
    bRF = [Buf() for _ in range(4)]
    RG0 = [Buf(), Buf()]
    RG1 = [Buf(), Buf()]
    RG2 = [Buf(), Buf()]

    xtok = RA
    diag = RA[:, 0:1984].bitcast(BF16).rearrange("p (k m) -> p k m", m=128)
    EB = RB[:].rearrange("p (h j) -> p h j", j=640)
    ext = RB[:, 0:4 * 2 * (15 + TP)].bitcast(F32).rearrange("p (g j) -> p g j", j=15 + TP)
    EBraw = RC[:].rearrange("p (a j) -> p a j", j=640)
    ptmp = RC[:, 0:2 * (15 + TP)].rearrange("p (a j) -> p a j", j=15 + TP)
    Kb = RD[:].rearrange("p (c j) -> p c j", j=1024)
    vtok = RD[:].bitcast(F32).rearrange("p (n j) -> p n j", j=512)
    Vb = RE[:].rearrange("p (n j) -> p n j", j=512)
    hid = RE[:].rearrange("p (a c j) -> p a c j", a=2, j=TP)
    actb = RF[:].rearrange("p (c j) -> p c j", j=TP)
    Qb = actb
    mb = actb
    vtb = actb
    ysb = RG[:, 0, :]
    dsb = RG[:, 1, :]
    nsb = RG[:, 2, :]
    ebuf = RG[:, 0, :].bitcast(BF16).rearrange("p (a j) -> p a j", j=TP)
    PTb = RG[:, 1, :].bitcast(BF16).rearrange("p (a j) -> p a j", j=TP)
    rdb = RG[:, 2, :]
    Gbb = RG[:, 0, :]
    Bbb = RG[:, 1, :]
    tsb = RG[:, 2, :]

    psum = es.enter_context(nc.psum_tensor("psum", [128, 8, 512], F32))
    bps = [Buf("ps%d" % i, excl=True) for i in range(8)]
    held = set()
    psrr = [0]

    def ps():
        for _ in range(16):
            i = psrr[0]
            psrr[0] = (i + 1) % 8
            if i not in held:
                return i
        raise RuntimeError("no psum bank")

    ringrr = [0]

    def mm(out, lhsT, rhs, start, stop, R, W, **kw):
        S.op("pe", lambda e: e.matmul(out, lhsT, rhs, start=start, stop=stop, **kw), R, W)

    def tr(out, in_, ident, R, W):
        S.op("pe", lambda e: e.transpose(out, in_, ident), R, W)

    def act(out, in_, func, R, W, bias=None, scale=None):
        kw = {}
        if bias is not None:
            kw["bias"] = bias
        if scale is not None:
            kw["scale"] = scale
        S.op("act", lambda e: e.activation(out, in_, func, **kw), R, W)

    def tt(out, in0, in1, op, R, W, eng="dve"):
        S.op(eng, lambda e: e.tensor_tensor(out, in0, in1, op), R, W)

    def ts(out, in0, s1, s2, op0, op1, R, W, eng="dve"):
        if op1 is None:
            S.op(eng, lambda e: e.tensor_scalar(out, in0, s1, None, op0), R, W)
        else:
            S.op(eng, lambda e: e.tensor_scalar(out, in0, s1, s2, op0, op1), R, W)

    def stt(out, in0, scalar, in1, op0, op1, R, W):
        S.op("dve", lambda e: e.scalar_tensor_tensor(out, in0, scalar, in1, op0, op1), R, W)

    def cp_(out, in_, R, W, eng="dve"):
        if eng == "act":
            act(out, in_, AF.Copy, R, W)
        else:
            S.op(eng, lambda e: e.tensor_copy(out, in_), R, W)

    def recip(out, in_, R, W):
        S.op("dve", lambda e: e.reciprocal(out, in_), R, W)

    def memset(ap, val, W, eng="dve"):
        S.op(eng, lambda e: e.memset(ap, val), (), W)

    wcache = {}

    def wslab(src, shape_free, key=None):
        i = ringrr[0]
        ringrr[0] = (i + 1) % RING
        n = 1
        for s_ in shape_free:
            n *= s_
        flat = ring[:, i, 0:n]
        v = flat
        if len(shape_free) == 2:
            v = v.rearrange("p (a b) -> p a b", b=shape_free[1])
        if key is not None and key in wcache:
            idx, cb = wcache[key]
            S.dma("sp", flat, WCL[idx // 92][idx % 92, :, 0:n], [cb], [bring[i]])
        else:
            S.dma("pool", v, src, (), [bring[i]])
            if key is not None and len(wcache) < NWC:
                idx = len(wcache)
                cb = Buf()
                wcache[key] = (idx, cb)
                S.dma("sp", WCL[idx // 92][idx % 92, :, 0:n], flat, [bring[i]], [cb])
        return v, bring[i]

    memset(ones_f[:], 1.0, [b_const])
    memset(ones_b[:], 1.0, [b_const])
    memset(tril[:], 0.0, [b_const])
    memset(tril[0:64, 0:64], 1.0 / 64, [b_const])
    memset(tril[64:128, 64:128], 1.0 / 64, [b_const])
    cp_(blk_b[:], tril[:], [b_const], [b_const])
    cp_(ones_r[:], ones_f[:], [b_const], [b_const])
    S.op("pool", lambda e: e.affine_select(out=ident_f[:], in_=ones_f[:], pattern=[[-1, 128]],
                                           compare_op=ALU.is_equal, fill=0.0, base=0,
                                           channel_multiplier=1), [b_const], [b_const])
    S.op("pool", lambda e: e.affine_select(out=antiI[:], in_=ones_f[:], pattern=[[1, 128]],
                                           compare_op=ALU.is_equal, fill=0.0, base=-127,
                                           channel_multiplier=1), [b_const], [b_const])
    S.op("pool", lambda e: e.affine_select(out=tril[:], in_=ones_f[:], pattern=[[-1, 128]],
                                           compare_op=ALU.is_ge, fill=0.0, base=0,
                                           channel_multiplier=1), [b_const], [b_const])
    cp_(ident_b[:], ident_f[:], [b_const], [b_const])
    memset(hconv[:], 0.0, b_hconv)
    memset(hpool[:], 0.0, b_hpool)
    S.dma("sp", flag[:], flag_d, (), [b_const])
    for g in range(4):
        w = 2 ** (g + 1)
        for t in range(16):
            c = 1.0 / min(w, t + 1)
            memset(invc[:, g, t:t + 1], c, [b_const])
            memset(invd[:, g, t:t + 1], 1.0 / w - c, [b_const])
    stt(invc[:], invd[:], flag[:, 0:1], invc[:], ALU.mult, ALU.add, [b_const], [b_const])

    stgrr = [0]

    def load_fm(dst, src_rows, R):
        i = stgrr[0]
        stgrr[0] = 1 - i
        S.dma("sp", stg[0:R, i, :], src_rows, (), [b_stg[i]])
        b = ps()
        tr(psum[:, b, 0:R], stg[0:R, i, :], ident_f[0:R, 0:R], [b_stg[i], b_const], [bps[b]])
        cp_(dst, psum[:, b, 0:R], [bps[b]], [b_par], eng="act")

    load_fm(fgT[:], final_g, 16)
    load_fm(csf[:, :, 0], cp, 16)
    load_fm(csf[:, :, 1], cs, 16)
    act(csb[:], csf[:], AF.Silu, [b_par], [b_par])
    for l in range(NL):
        load_fm(n1T[:, l, :], n1[l], 16)
        load_fm(n2T[:, l, :], n2[l], 16)
        load_fm(cbT[:, l, :], conv_b[l], 4)
        load_fm(ggT[:, l, :], gn_g[l], 4)
        load_fm(gbT[:, l, :], gn_b[l], 4)
        load_fm(pscT[:, l, :], pool_scale[l], 4)
        for c in range(4):
            load_fm(cwT[:, l, c, :], conv_w[l][:, c * 128:(c + 1) * 128], 31)
        for g in range(4):
            i = stgrr[0]
            stgrr[0] = 1 - i
            S.dma("sp", stg[:, i, :], sg_w[l, g], (), [b_stg[i]])
            tt(stg[:, i, :], stg[:, i, :], tril[:], ALU.mult, [b_stg[i], b_const], [b_stg[i]])
            b = ps()
            tr(psum[:, b, 0:128], stg[:, i, :], ident_f[:], [b_stg[i], b_const], [bps[b]])
            cp_(WT[:, l, g, :], psum[:, b, 0:128], [bps[b]], [b_par], eng="act")
        rbs = RA[0:8, 0:257]
        Gsb = RA[0:8, 512:1279]
        S.dma("sp", rbs, rel_bias[l], (), [bRA])
        memset(Gsb, 0.0, [bRA])
        ts(Gsb[:, 0:511], Gsb[:, 0:511], rbs[:, 0:1], None, ALU.add, None, [bRA], [bRA])
        cp_(Gsb[:, 511:767], rbs[:, 0:256], [bRA], [bRA])
        S.dma("sp", Gd[l * 8 * 767:(l + 1) * 8 * 767].rearrange("(h t) -> h t", t=767), Gsb,
              [bRA], [bG[l]])

    for l in range(NL):
        load_fm(adabT[:], ada_b[l], 96)
        bm = ps()
        held.add(bm)
        awv = ada_w[l].rearrange("(c p) m -> p c m", p=128)
        for si in range(48):
            slot, bs = wslab(awv[:, :, si * 256:(si + 1) * 256], [16, 256])
            for mc in range(2):
                m = si * 2 + mc
                for k in range(16):
                    mm(psum[:, bm, 2 * m:2 * m + 2], slot[:, k, mc * 128:(mc + 1) * 128], csb[:, k, :],
                       k == 0, k == 15, [bs, b_par], [bps[bm]], skip_group_check=True)
        pm = psum[:, bm, 0:192].rearrange("p (m s) -> p m s", s=2)
        for s_ in range(2):
            tt(modl[:, :, s_], pm[:, :, s_], adabT[:], ALU.add, [bps[bm], b_par], [b_par])
        held.discard(bm)
        for s_ in range(2):
            stt(AD[:, l, 0, :, s_], modl[:, 16:32, s_], 1.0, n1T[:, l, :], ALU.add, ALU.mult, [b_par], [b_par])
            cp_(AD[:, l, 1, :, s_], modl[:, 0:16, s_], [b_par], [b_par])
            cp_(AD[:, l, 2, :, s_], modl[:, 32:48, s_], [b_par], [b_par])
            stt(AD[:, l, 3, :, s_], modl[:, 64:80, s_], 1.0, n2T[:, l, :], ALU.add, ALU.mult, [b_par], [b_par])
            cp_(AD[:, l, 4, :, s_], modl[:, 48:64, s_], [b_par], [b_par])
            cp_(AD[:, l, 5, :, s_], modl[:, 80:96, s_], [b_par], [b_par])

    evrr = [0]

    def evac_eng():
        evrr[0] ^= 1
        return "act" if evrr[0] else "dve"

    def load_x(src_rows, T):
        nb = max(T // 128, 1)
        rows = min(T, 128)
        for tb in range(nb):
            S.dma("sp", xtok[0:rows, :], src_rows[tb * 128:tb * 128 + rows, :], (), [bRA])
            for cg in range(4):
                b = ps()
                for j in range(4):
                    c = cg * 4 + j
                    tr(psum[:, b, j * rows:(j + 1) * rows], xtok[0:rows, c * 128:(c + 1) * 128],
                       ident_f[0:rows, 0:rows], [bRA, b_const], [bps[b]])
                cp_(xT[:, cg * 4:cg * 4 + 4, tb * 128:tb * 128 + rows],
                    psum[:, b, 0:4 * rows].rearrange("p (j t) -> p j t", t=rows),
                    [bps[b]], bx[cg * 4:cg * 4 + 4], eng=evac_eng())

    def rms_stats(T):
        b = ps()
        for c in range(16):
            act(sq[:, c % 2, 0:T], xT[:, c, 0:T], AF.Square, [bx[c]], [bsq[c % 2]])
            mm(psum[:, b, 0:T], ones_b[:], sq[:, c % 2, 0:T], c == 0, c == 15, [bsq[c % 2], b_const], [bps[b]])
        act(sd[:, 0:T], psum[:, b, 0:T], AF.Sqrt, [bps[b]], [bsd], bias=EPS_AP[:, 0:1], scale=1.0 / D)
        recip(rstd[:, 0:T], sd[:, 0:T], [bsd], [brstd])

    def norm_to_h(T, a_ap, b_ap):
        rms_stats(T)
        for c in range(16):
            if c % 2 == 0:
                t_ = (c // 2) % 2
                stt(tmp[:, t_, 0:T], xT[:, c, 0:T], a_ap[:, c:c + 1], rstd[:, 0:T], ALU.mult, ALU.mult,
                    [bx[c], brstd, b_par], [btmp[t_]])
                act(hT[:, c, 0:T], tmp[:, t_, 0:T], AF.Identity, [btmp[t_], b_par], [bh[c]],
                    bias=b_ap[:, c:c + 1], scale=1.0)
            else:
                t_ = 2 + (c // 2) % 2
                tt(tmp[:, t_, 0:T], xT[:, c, 0:T], rstd[:, 0:T], ALU.mult, [bx[c], brstd], [btmp[t_]], eng="pool")
                act(hT[:, c, 0:T], tmp[:, t_, 0:T], AF.Identity, [btmp[t_], b_par], [bh[c]],
                    bias=b_ap[:, c:c + 1], scale=a_ap[:, c:c + 1])

    def proj_fm(wl, col0, nchunk, T, consume, wkey=None):
        wv = wl.rearrange("(c p) m -> p c m", p=128)
        for sj in range(nchunk // 2):
            slot, bs = wslab(wv[:, :, col0 + sj * 256:col0 + (sj + 1) * 256], [16, 256], key=(wkey, col0 + sj * 256))
            banks = []
            for mc in range(2):
                b = ps()
                for k in range(16):
                    mm(psum[:, b, 0:T], slot[:, k, mc * 128:(mc + 1) * 128], hT[:, k, 0:T], k == 0, k == 15,
                       [bs, bh[k]], [bps[b]])
                banks.append(b)
            for mc in range(2):
                consume(sj * 2 + mc, banks[mc])

    def proj_tm(wl, col0, T, consume, wkey=None):
        wv = wl.rearrange("(c p) m -> p c m", p=128)
        nb = max(T // 128, 1)
        rows = min(T, 128)
        banks = [ps() for _ in range(nb)]
        for i_ in banks:
            held.add(i_)
        for half in range(2):
            slot, bs = wslab(wv[:, :, col0 + half * 256:col0 + (half + 1) * 256], [16, 256], key=(wkey, col0 + half * 256))
            for tb in range(nb):
                b = banks[tb]
                for k in range(16):
                    mm(psum[0:rows, b, half * 256:(half + 1) * 256], hT[:, k, tb * 128:tb * 128 + rows],
                       slot[:, k, :], k == 0, k == 15, [bs, bh[k]], [bps[b]])
        for tb in range(nb):
            consume(tb, banks[tb])
            held.discard(banks[tb])

    def out_rows_fm(dst, src_fm, n, R):
        b = ps()
        tr(psum[0:n, b, 0:128], src_fm, ident_f[:], R + [b_const], [bps[b]])
        cp_(ostage[0:n, 0:128], psum[0:n, b, 0:128], [bps[b]], [b_ostage], eng="act")
        S.dma("sp", dst, ostage[0:n, 0:128], [b_ostage], ())

    def layer(l, T, sidx, mode, first, halo_save, first_real, want_state, O):
        sample = sidx == 1
        a1 = AD[:, l, 0, :, sidx]
        b1 = AD[:, l, 1, :, sidx]
        g1 = AD[:, l, 2, :, sidx]
        a2 = AD[:, l, 3, :, sidx]
        b2_ = AD[:, l, 4, :, sidx]
        g2 = AD[:, l, 5, :, sidx]
        wl = w_in[l]
        full = mode == "full"
        if STOP <= 0:
            return
        norm_to_h(T, a1, b1)
        if STOP <= 1:
            return
        if full and SUB >= 2 and STOP > 2:
            for hd in range(8):
                src = bass.AP(tensor=Gd.tensor, offset=(l * 8 + hd) * 767, ap=[[1, 128], [128, 5], [1, 128]])
                S.dma("sp", EBraw[:, hd % 2, :].rearrange("p (a k) -> p a k", k=128), src, [bG[l]], [bRC[hd % 2]])
                b = ps()
                for jb in range(4):
                    mm(psum[:, b, jb * 128:(jb + 1) * 128], EBraw[:, hd % 2, (4 - jb) * 128:(5 - jb) * 128], antiI[:], True, True,
                       [bRC[hd % 2], b_const], [bps[b]])
                act(EB[:, hd, 0:512], psum[:, b, :], AF.Exp, [bps[b]], [bRB[hd]])
                b = ps()
                mm(psum[:, b, 0:128], EBraw[:, hd % 2, 0:128], antiI[:], True, True, [bRC[hd % 2], b_const], [bps[b]])
                act(EB[:, hd, 512:640], psum[:, b, 0:128], AF.Exp, [bps[b]], [bRB[hd]])
            memset(EB[0:64, :, 576:640], 0.0, bRB)
            memset(EB[64:128, :, 0:64], 0.0, bRB)

        if sample:
            S.dma("sp", xtok[0:30, 0:512], cconv[l], (), [bRA])
            for c in range(4):
                b = ps()
                tr(psum[:, b, 0:30], xtok[0:30, c * 128:(c + 1) * 128], ident_f[0:30, 0:30], [bRA, b_const], [bps[b]])
                cp_(uext[:, c, 0:30], psum[:, b, 0:30], [bps[b]], [buext[c]], eng="act")
        else:
            for c in range(4):
                cp_(uext[:, c, 0:30], hconv[:, l, c, :], [b_hconv[l]], [buext[c]])
        zab = {}

        def cons_a(j, b):
            zab[j] = b

        proj_fm(wl, 0, 4, T, cons_a, wkey=("win", l))
        for j in range(4):
            held.add(zab[j])

        def cons_g(j, b):
            act(tmp[:, j % 2, 0:T], psum[:, b, 0:T], AF.Sigmoid, [bps[b]], [btmp[j % 2]])
            tt(uext[:, j, 30:30 + T], psum[:, zab[j], 0:T], tmp[:, j % 2, 0:T], ALU.mult,
               [bps[zab[j]], btmp[j % 2]], [buext[j]])
            if want_state:
                tt(u32[:, j, 0:30], psum[:, zab[j], T - 30:T], tmp[:, j % 2, T - 30:T], ALU.mult,
                   [bps[zab[j]], btmp[j % 2]], [b_u32])
            held.discard(zab[j])

        proj_fm(wl, 512, 4, T, cons_g, wkey=("win", l))
        if want_state:
            for c in range(4):
                out_rows_fm(O["conv"][l][:, c * 128:(c + 1) * 128], u32[:, c, 0:30], 30, [b_u32])
        if not sample:
            for c in range(4):
                if halo_save:
                    ts(hconv[:, l, c, :], uext[:, c, T:T + 30], flag[:, 0:1], None, ALU.mult, None,
                       [buext[c], b_const], [b_hconv[l]])
                else:
                    cp_(hconv[:, l, c, :], uext[:, c, T:T + 30], [buext[c]], [b_hconv[l]])
        if full:
            for c in range(4):
                for k in range(31):
                    ts(diag[:, k, :], ident_b[:], cwT[:, l, c, k:k + 1], None, ALU.mult, None,
                       [b_const, b_par], [bRA])
                b = ps()
                for k in range(31):
                    mm(psum[:, b, 0:T], diag[:, k, :], uext[:, c, k:k + T], k == 0, k == 30,
                       [bRA, buext[c]], [bps[b]])
                act(ysb[:, 0:T], psum[:, b, 0:T], AF.Identity, [bps[b], b_par], RG0,
                    bias=cbT[:, l, c:c + 1], scale=1.0)
                act(ybf[:, 0:T], psum[:, b, 0:T], AF.Identity, [bps[b], b_par], [b_ybf],
                    bias=cbT[:, l, c:c + 1], scale=1.0)
                b2 = ps()
                mm(psum[:, b2, 0:T], blk_b[:], ybf[:, 0:T], True, True, [b_ybf, b_const], [bps[b2]])
                tt(dsb[:, 0:T], RG[:, 0, 0:T], psum[:, b2, 0:T], ALU.subtract, RG0 + [bps[b2]], RG1)
                act(sq[:, 0, 0:T], dsb[:, 0:T], AF.Square, RG1, [bsq[0]])
                b3 = ps()
                mm(psum[:, b3, 0:T], blk_b[:], sq[:, 0, 0:T], True, True, [bsq[0], b_const], [bps[b3]])
                act(sd[:, 0:T], psum[:, b3, 0:T], AF.Sqrt, [bps[b3]], [bsd], bias=EPS_AP[:, 0:1], scale=1.0)
                recip(nsb[:, 0:T], sd[:, 0:T], [bsd], RG2)
                tt(dsb[:, 0:T], dsb[:, 0:T], nsb[:, 0:T], ALU.mult, RG1 + RG2, RG1)
                act(actb[:, c, 0:T], dsb[:, 0:T], AF.Silu, RG1 + [b_par], [bRF[c]],
                    bias=gbT[:, l, c:c + 1], scale=ggT[:, l, c:c + 1])
            slot, bs = wslab(conv_pw[l].rearrange("(c p) m -> p c m", p=128), [4, 512], key=("pw", l))
            for m in range(4):
                b = ps()
                for kc in range(4):
                    mm(psum[:, b, 0:T], slot[:, kc, m * 128:(m + 1) * 128], actb[:, kc, 0:T], kc == 0, kc == 3,
                       [bs, bRF[kc]], [bps[b]])
                cp_(mixT[:, m, 0:T], psum[:, b, 0:T], [bps[b]], [bmix[m]], eng=evac_eng())

        if STOP <= 2:
            return
        H = 512
        if sample:
            S.dma("sp", xtok[:].rearrange("p (n f) -> p n f", f=512),
                  ck[l].rearrange("(n p) f -> p n f", p=128), (), [bRA])
            for c in range(4):
                b = ps()
                for kb in range(4):
                    tr(psum[:, b, kb * 128:(kb + 1) * 128], xtok[:, kb * 512 + c * 128:kb * 512 + (c + 1) * 128],
                       ident_f[:], [bRA, b_const], [bps[b]])
                cp_(Kb[:, c, 0:512], psum[:, b, :], [bps[b]], [bRD[c]], eng=evac_eng())
            S.dma("sp", xtok[:].rearrange("p (n f) -> p n f", f=512),
                  cv[l].rearrange("(n p) f -> p n f", p=128), (), [bRA])
            cp_(Vb[:, 0:4, :], xtok[:].rearrange("p (n f) -> p n f", f=512), [bRA], bRE[0:4])
        elif not first:
            S.dma("sp", Kb[:, :, 0:512], Khist[l].rearrange("p (c j) -> p c j", j=512), [bKh[l]], bRD)
            S.dma("sp", Vb[:, 0:4, :], Vhist[l].rearrange("p (n j) -> p n j", j=512), [bVh[l]], bRE[0:4])
        if full and (KB & 1):
            def cons_q(j, b):
                act(Qb[:, j, 0:T], psum[:, b, 0:T], AF.Copy, [bps[b]], [bRF[j]], scale=0.125)

            proj_fm(wl, 1024, 4, T, cons_q, wkey=("win", l))

        def cons_k(j, b):
            cp_(Kb[:, j, H:H + T], psum[:, b, 0:T], [bps[b]], [bRD[j]])
            if want_state and (KB & 4):
                cp_(tmp[:, j % 2, 0:T], psum[:, b, 0:T], [bps[b]], [btmp[j % 2]], eng="act")
                nb = max(T // 128, 1)
                rows = min(T, 128)
                b4 = ps()
                for tb in range(nb):
                    tr(psum[0:rows, b4, tb * 128:(tb + 1) * 128], tmp[:, j % 2, tb * 128:tb * 128 + rows],
                       ident_f[:], [btmp[j % 2], b_const], [bps[b4]])
                cp_(ostage[0:rows, 0:nb * 128], psum[0:rows, b4, 0:nb * 128], [bps[b4]], [b_ostage], eng="act")
                if not (KB & 64):
                    pass
                elif sample:
                    S.dma("sp", O["k"][l][:, j * 128:(j + 1) * 128], ostage[0:rows, 0:128], [b_ostage], ())
                else:
                    S.dma("sp", O["k"][l][:, j * 128:(j + 1) * 128].rearrange("(tb p) f -> p tb f", p=128),
                          ostage[:, :].rearrange("p (tb f) -> p tb f", f=128), [b_ostage], ())

        if KB & 2:
            proj_fm(wl, 1536, 4, T, cons_k, wkey=("win", l))
        rowsT = min(T, 128)

        def cons_v(tb, b):
            cp_(Vb[0:rowsT, 4 + tb, :], psum[0:rowsT, b, :], [bps[b]], [bRE[4 + tb]])
            if want_state and (KB & 16):
                cp_(ostage[0:rowsT, :], psum[0:rowsT, b, :], [bps[b]], [b_ostage], eng="act")
                S.dma("sp", O["v"][l][tb * 128:tb * 128 + rowsT, :], ostage[0:rowsT, :], [b_ostage], ())

        if KB & 8:
            proj_tm(wl, 2048, T, cons_v, wkey=("win", l))
        if not sample and (KB & 32):
            S.dma("sp", Khist[l].rearrange("p (c j) -> p c j", j=512), Kb[:, :, H:H + T], bRD, [bKh[l]])
            S.dma("sp", Vhist[l].rearrange("p (n j) -> p n j", j=512), Vb[:, 4:8, :], bRE[4:8], [bVh[l]])

        if full and SUB >= 2:
            blist = []
            if sample:
                for b_ in range(4):
                    blist.append((b_ * 128, 128, b_, 8 - 2 * b_, 0, T, False))
                blist.append((512, T, 4, 0, 0, T, False))
            else:
                for b_ in range(8):
                    if first and b_ < 4:
                        continue
                    n0 = max(0, 2 * b_ - 8)
                    n1_ = min(7, 2 * b_ + 1)
                    blist.append((b_ * 128, 128, b_, n0 - 2 * b_ + 8, 64 * n0, 64 * (n1_ - n0 + 1),
                                  first_real and b_ < 4))
            jj = 0
            for hp in range(4 if SUB >= 3 else 0):
                bo = ps()
                held.add(bo)
                bd = ps()
                held.add(bd)
                items = [(s_, bi) for s_ in range(2) for bi in range(len(blist))]

                def stage1(s_, bi):
                    nonlocal jj
                    kc0, nk, vb_, i0, q0, N, uf = blist[bi]
                    hd = 2 * hp + s_
                    pr = slice(64 * s_, 64 * s_ + 64)
                    b = ps()
                    mm(psum[0:nk, b, 0:N], Kb[pr, hp, kc0:kc0 + nk], Qb[pr, hp, q0:q0 + N], True, True,
                       [bRD[hp], bRF[hp]], [bps[b]])
                    e_ = jj % 2
                    jj += 1
                    act(ebuf[0:nk, e_, 0:N], psum[0:nk, b, 0:N], AF.Exp, [bps[b]], [RG0[e_]])
                    if uf:
                        stt(PTb[0:nk, e_, 0:N], ebuf[0:nk, e_, 0:N], flag[0:nk, 0:1],
                            EB[0:nk, hd, 64 * i0:64 * i0 + N], ALU.mult, ALU.mult,
                            [RG0[e_], bRB[hd], b_const], [RG1[e_]])
                    else:
                        tt(PTb[0:nk, e_, 0:N], ebuf[0:nk, e_, 0:N], EB[0:nk, hd, 64 * i0:64 * i0 + N], ALU.mult,
                           [RG0[e_], bRB[hd]], [RG1[e_]])
                    return (s_, bi, e_)

                def stage2(s_, bi, e_):
                    kc0, nk, vb_, i0, q0, N, uf = blist[bi]
                    hd = 2 * hp + s_
                    pr = slice(64 * s_, 64 * s_ + 64)
                    fst = bi == 0
                    lst = bi == len(blist) - 1
                    mm(psum[pr, bo, q0:q0 + N], Vb[0:nk, vb_, hd * 64:(hd + 1) * 64], PTb[0:nk, e_, 0:N],
                       fst, lst, [bRE[vb_], RG1[e_]], [bps[bo]], skip_group_check=True)
                    mm(psum[pr, bd, q0:q0 + N], ones_b[0:nk, 0:64], PTb[0:nk, e_, 0:N],
                       fst, lst, [b_const, RG1[e_]], [bps[bd]], skip_group_check=True)

                prev = None
                for (s_, bi) in items:
                    cur = stage1(s_, bi)
                    if prev is not None:
                        stage2(*prev)
                    prev = cur
                stage2(*prev)
                recip(rdb[:, 0:T], psum[:, bd, 0:T], [bps[bd]], RG2)
                tt(mixT[:, 4 + hp, 0:T], psum[:, bo, 0:T], rdb[:, 0:T], ALU.mult, [bps[bo]] + RG2, [bmix[4 + hp]])
                held.discard(bo)
                held.discard(bd)

        if STOP <= 3:
            return
        if sample:
            S.dma("sp", xtok[0:15, 0:512], cpool[l], (), [bRA])
            for g in range(4):
                b = ps()
                tr(psum[:, b, 0:15], xtok[0:15, g * 128:(g + 1) * 128], ident_f[0:15, 0:15], [bRA, b_const], [bps[b]])
                cp_(ext[:, g, 0:15], psum[:, b, 0:15], [bps[b]], EXTB[g], eng="act")
        else:
            for g in range(4):
                cp_(ext[:, g, 0:15], hpool[:, l, g, :], [b_hpool[l]], EXTB[g])

        def cons_p(j, b):
            cp_(ext[:, j, 15:15 + T], psum[:, b, 0:T], [bps[b]], EXTB[j], eng="act")

        proj_fm(wl, 2560, 4, T, cons_p, wkey=("win", l))
        if not sample:
            for g in range(4):
                if halo_save:
                    ts(hpool[:, l, g, :], ext[:, g, T:T + 15], flag[:, 0:1], None, ALU.mult, None,
                       EXTB[g] + [b_const], [b_hpool[l]])
                else:
                    cp_(hpool[:, l, g, :], ext[:, g, T:T + 15], EXTB[g], [b_hpool[l]])
        if want_state:
            for g in range(4):
                out_rows_fm(O["pool"][l][:, g * 128:(g + 1) * 128], ext[:, g, T:T + 15], 15, EXTB[g])
        if full:
            E = 15 + T
            slot, bs = wslab(pool_w[l].rearrange("g c d -> c g d"), [4, 128], key=("poolw", l))
            for g in range(4):
                e_ = ext[:, g, :]
                t1 = ptmp[:, 0, :]
                t2 = ptmp[:, 1, :]
                tt(t1[:, 1:E], e_[:, 1:E], e_[:, 0:E - 1], ALU.add, EXTB[g], PTMPB[0])
                win = t1
                wb = PTMPB[0]
                if g >= 1:
                    tt(t2[:, 3:E], t1[:, 3:E], t1[:, 1:E - 2], ALU.add, PTMPB[0], PTMPB[1])
                    win = t2
                    wb = PTMPB[1]
                if g >= 2:
                    tt(t1[:, 7:E], t2[:, 7:E], t2[:, 3:E - 4], ALU.add, PTMPB[1], PTMPB[0])
                    win = t1
                    wb = PTMPB[0]
                if g >= 3:
                    tt(t2[:, 15:E], t1[:, 15:E], t1[:, 7:E - 8], ALU.add, PTMPB[0], PTMPB[1])
                    win = t2
                    wb = PTMPB[1]
                w_ = 2 ** (g + 1)
                stt(mb[:, g, 0:T], win[:, 15:E], 1.0 / w_, e_[:, 15:E], ALU.mult, ALU.subtract,
                    wb + EXTB[g], [bRF[g]])
                if first_real:
                    tt(t16[:, :], win[:, 15:31], invc[:, g, :], ALU.mult, wb + [b_const], [b_small])
                    tt(mb[:, g, 0:16], t16[:, :], e_[:, 15:31], ALU.subtract, [b_small] + EXTB[g], [bRF[g]])
                b = ps()
                mm(psum[:, b, 0:T], slot[:, g, :], mb[:, g, 0:T], True, True, [bs, bRF[g]], [bps[b]])
                act(mixT[:, 8 + g, 0:T], psum[:, b, 0:T], AF.Copy, [bps[b], b_par], [bmix[8 + g]],
                    scale=pscT[:, l, g:g + 1])

        if STOP <= 4:
            return
        if full:
            S.dma("sp", b2f[:], sg_b[l], (), [b_b2])
            cp_(b2r[:], b2f[:], [b_b2], [b_b2])
            S.dma("sp", Gbb[:, :], sg_g[l].partition_broadcast(128), (), RG0)
            S.dma("sp", Bbb[:, :], sg_bn[l].partition_broadcast(128), (), RG1)
            nb = max(T // 128, 1)

            def cons_s(tb, b):
                S.op("dve", lambda e: e.bn_stats(st6[0:rowsT, :], psum[0:rowsT, b, :]), [bps[b]], [b_small])
                S.op("dve", lambda e: e.bn_aggr(mv[0:rowsT, :], st6[0:rowsT, :]), [b_small], [b_small])
                act(rv[0:rowsT, 0:1], mv[0:rowsT, 1:2], AF.Sqrt, [b_small], [b_small], bias=EPS_AP[0:rowsT, 0:1], scale=1.0)
                recip(rv[0:rowsT, 1:2], rv[0:rowsT, 0:1], [b_small], [b_small])
                ts(vtok[0:rowsT, tb, :], psum[0:rowsT, b, :], mv[0:rowsT, 0:1], rv[0:rowsT, 1:2], ALU.subtract, ALU.mult,
                   [bps[b], b_small], [bRD[tb]])
                tt(vtok[0:rowsT, tb, :], vtok[0:rowsT, tb, :], Gbb[0:rowsT, :], ALU.mult, [bRD[tb]] + RG0, [bRD[tb]])
                tt(vtok[0:rowsT, tb, :], vtok[0:rowsT, tb, :], Bbb[0:rowsT, :], ALU.add, [bRD[tb]] + RG1, [bRD[tb]])
                cp_(vtb[0:rowsT, tb, :], vtok[0:rowsT, tb, :], [bRD[tb]], [bRF[tb]], eng="act")
                if sample:
                    S.dma("sp", O["sgv"][l], vtok[0:rowsT, 0, :], [bRD[0]], ())

            proj_tm(wl, 3584, T, cons_s, wkey=("win", l))
            wvu = wl.rearrange("(c p) m -> p c m", p=128)
            uslot = None
            for g in range(4):
                bsb = ps()
                for n_ in range(nb):
                    mm(psum[:, bsb, n_ * 128:n_ * 128 + rowsT], vtb[0:rowsT, n_, g * 128:(g + 1) * 128],
                       WT[0:rowsT, l, g, 0:rowsT], True, False, [bRF[n_], b_par], [bps[bsb]], skip_group_check=True)
                    mm(psum[:, bsb, n_ * 128:n_ * 128 + rowsT], ones_r[0:1, :],
                       b2r[0:1, g * 128:g * 128 + rowsT], False, True,
                       [b_const, b_b2], [bps[bsb]], skip_group_check=True)
                cp_(tsb[:, 0:T], psum[:, bsb, 0:T], [bps[bsb]], RG2, eng="act")
                if g % 2 == 0:
                    uslot, ubs = wslab(wvu[:, :, 3072 + (g // 2) * 256:3072 + (g // 2 + 1) * 256], [16, 256], key=(("win", l), 3072 + (g // 2) * 256))
                b = ps()
                for k in range(16):
                    mm(psum[:, b, 0:T], uslot[:, k, (g % 2) * 128:(g % 2 + 1) * 128], hT[:, k, 0:T], k == 0, k == 15,
                       [ubs, bh[k]], [bps[b]])
                tt(mixT[:, 12 + g, 0:T], psum[:, b, 0:T], tsb[:, 0:T], ALU.mult, [bps[b]] + RG2, [bmix[12 + g]])

        if not full or STOP <= 5:
            return
        wv = w_out[l].rearrange("(c p) m -> p c m", p=128)
        for sj in range(8):
            slot, bs = wslab(wv[:, :, sj * 256:(sj + 1) * 256], [16, 256], key=("wout", l, sj))
            for mc in range(2):
                m = sj * 2 + mc
                b = ps()
                for k in range(16):
                    mm(psum[:, b, 0:T], slot[:, k, mc * 128:(mc + 1) * 128], mixT[:, k, 0:T], k == 0, k == 15,
                       [bs, bmix[k]], [bps[b]])
                stt(xT[:, m, 0:T], psum[:, b, 0:T], g1[:, m:m + 1], xT[:, m, 0:T], ALU.mult, ALU.add,
                    [bps[b], bx[m], b_par], [bx[m]])
        if STOP <= 6:
            return
        norm_to_h(T, a2, b2_)
        gv = wg[l].rearrange("(c p) m -> p c m", p=128)
        uv = wu[l].rearrange("(c p) m -> p c m", p=128)
        dv = wd[l].rearrange("(j p) m -> p j m", p=128)
        NG = NFF // 4

        def gu(gi):
            a_ = gi % 2
            for half in range(2):
                gs, gbs = wslab(gv[:, :, gi * 512 + half * 256:gi * 512 + (half + 1) * 256], [16, 256], key=("wg", l, gi, half))
                us, ubs_ = wslab(uv[:, :, gi * 512 + half * 256:gi * 512 + (half + 1) * 256], [16, 256], key=("wu", l, gi, half))
                for mc in range(2):
                    j = half * 2 + mc
                    bg = ps()
                    for k in range(16):
                        mm(psum[:, bg, 0:T], gs[:, k, mc * 128:(mc + 1) * 128], hT[:, k, 0:T], k == 0, k == 15,
                           [gbs, bh[k]], [bps[bg]])
                    bu = ps()
                    for k in range(16):
                        mm(psum[:, bu, 0:T], us[:, k, mc * 128:(mc + 1) * 128], hT[:, k, 0:T], k == 0, k == 15,
                           [ubs_, bh[k]], [bps[bu]])
                    act(sl[:, j % 2, 0:T], psum[:, bg, 0:T], AF.Silu, [bps[bg]], [bsl[j % 2]])
                    tt(hid[:, a_, j, 0:T], psum[:, bu, 0:T], sl[:, j % 2, 0:T], ALU.mult, [bps[bu], bsl[j % 2]],
                       [bRE[a_ * 4 + j]])

        def down(gi):
            a_ = gi % 2
            d0, db0 = wslab(dv[:, gi * 4:gi * 4 + 2, :], [2, 2048], key=("wd", l, gi, 0))
            d1, db1 = wslab(dv[:, gi * 4 + 2:gi * 4 + 4, :], [2, 2048], key=("wd", l, gi, 1))
            for m in range(16):
                b = ps()
                for j in range(4):
                    ds_, dbs = (d0, db0) if j < 2 else (d1, db1)
                    mm(psum[:, b, 0:T], ds_[:, j % 2, m * 128:(m + 1) * 128], hid[:, a_, j, 0:T], j == 0, j == 3,
                       [dbs, bRE[a_ * 4 + j]], [bps[b]])
                stt(xT[:, m, 0:T], psum[:, b, 0:T], g2[:, m:m + 1], xT[:, m, 0:T], ALU.mult, ALU.add,
                    [bps[b], bx[m], b_par], [bx[m]])

        gu(0)
        for gi in range(1, NG):
            gu(gi)
            down(gi - 1)
        down(NG - 1)

    def final_out(T, dst_rows):
        rms_stats(T)
        nb = max(T // 128, 1)
        rows = min(T, 128)
        for c in range(16):
            stt(xT[:, c, 0:T], xT[:, c, 0:T], fgT[:, c:c + 1], rstd[:, 0:T], ALU.mult, ALU.mult,
                [bx[c], brstd, b_par], [bx[c]])
        for tb in range(nb):
            for cg in range(4):
                b = ps()
                for j in range(4):
                    c = cg * 4 + j
                    tr(psum[0:rows, b, j * 128:(j + 1) * 128], xT[:, c, tb * 128:tb * 128 + rows], ident_f[:],
                       [bx[c], b_const], [bps[b]])
                cp_(xtok[0:rows, cg * 512:(cg + 1) * 512], psum[0:rows, b, :], [bps[b]], [bRA], eng=evac_eng())
            S.dma("sp", dst_rows[tb * 128:tb * 128 + rows, :], xtok[0:rows, :], [bRA], ())

    EPS_T = sb("eps_t", [128, 1], F32)
    EPS_AP = EPS_T
    memset(EPS_T[:], EPS, [b_const])

    Op = {"conv": convp, "k": kp, "v": vp, "pool": poolp}
    Os = {"conv": convs, "k": ks, "v": vs, "pool": pools, "sgv": sgv}
    for ti in range(NT):
        load_x(xp[ti * TP:(ti + 1) * TP, :], TP)
        halo = ti < NH
        for l in range(NL):
            if halo:
                if l > ti:
                    continue
                mode = "full" if l < ti else "kv"
            else:
                mode = "full"
            layer(l, TP, 0, mode, first=(ti == 0), halo_save=halo, first_real=(ti == NH and NH > 0),
                  want_state=(ti == NT - 1), O=Op)
        if not halo:
            final_out(TP, yp[(ti - NH) * TP:(ti - NH + 1) * TP, :])
    if do_sample:
        load_x(xs, TS)
        for l in range(NL):
            layer(l, TS, 1, "full", first=False, halo_save=False, first_real=False, want_state=True, O=Os)
        final_out(TS, ys)

    S.emit(es)
    es.close()
    return nc


_CACHE = {}


def kernel(x_prompt, x_sample, c_prompt, c_sample, cache_conv, cache_k, cache_v, cache_pool,
           ada_w, ada_b, norm1_g, norm2_g, w_in, conv_w, conv_b, conv_gn_g, conv_gn_b, conv_pw,
           rel_bias, pool_w, pool_scale, sg_ln_g, sg_ln_b, sg_w, sg_b, w_out,
           ffn_gate, ffn_up, ffn_down, final_g):
    dbg = os.environ.get("KDEBUG")
    if dbg:
        NH, NR, NL, smp = [int(v) for v in dbg.split(",")]
    else:
        NH, NR, NL, smp = 4, 8, 4, 1
    key = (NH, NR, NL, smp)
    if key not in _CACHE:
        _CACHE[key] = build_program(NH, NR, NL, bool(smp))
    nc = _CACHE[key]
    f = lambda a: np.ascontiguousarray(np.asarray(a, dtype=np.float32))
    fl = lambda a: np.ascontiguousarray(np.asarray(a, dtype=np.float32)[:NL])
    NT = NH + NR
    shared = {
        "ada_w": fl(ada_w), "ada_b": fl(ada_b).reshape(NL, 96, 128), "n1": fl(norm1_g).reshape(NL, 16, 128),
        "n2": fl(norm2_g).reshape(NL, 16, 128), "w_in": fl(w_in), "conv_w": fl(conv_w),
        "conv_b": fl(conv_b).reshape(NL, 4, 128), "gn_g": fl(conv_gn_g).reshape(NL, 4, 128),
        "gn_b": fl(conv_gn_b).reshape(NL, 4, 128), "conv_pw": fl(conv_pw), "rel_bias": fl(rel_bias),
        "pool_w": fl(pool_w), "pool_scale": fl(pool_scale).reshape(NL, 4, 128), "sg_g": fl(sg_ln_g),
        "sg_bn": fl(sg_ln_b), "sg_w": fl(sg_w), "sg_b": fl(sg_b).reshape(NL, 1, 512), "w_out": fl(w_out),
        "wg": fl(ffn_gate), "wu": fl(ffn_up), "wd": fl(ffn_down), "final_g": f(final_g).reshape(16, 128),
    }
    xpr = f(x_prompt)
    xsm = f(x_sample)
    in_maps = []
    for i in range(8):
        b, half = i // 2, i % 2
        start = half * 4096 - NH * TP
        xin = np.zeros((NT * TP, D), np.float32)
        lo = max(start, 0)
        hi = min(start + NT * TP, 8192)
        xin[lo - start:hi - start] = xpr[b, lo:hi]
        m = dict(shared)
        m.update({
            "xp": xin, "flag": np.full((128, 1), float(half), np.float32), "xs": xsm[i],
            "cp": f(c_prompt)[b].reshape(16, 128), "cs": f(c_sample)[i].reshape(16, 128),
            "cconv": f(cache_conv)[:NL, i], "ck": f(cache_k)[:NL, i].reshape(NL, 512, 512),
            "cv": f(cache_v)[:NL, i].reshape(NL, 512, 512), "cpool": f(cache_pool)[:NL, i],
        })
        in_maps.append({k: np.ascontiguousarray(v) for k, v in m.items()})
    res = run_bass_kernel_spmd(nc, in_maps, core_ids=list(range(8)))
    R = res.results
    B = 4
    y_prompt = np.zeros((B, 8192, D), np.float32)
    for i in range(8):
        b, half = i // 2, i % 2
        n = min(NR * TP, 4096)
        y_prompt[b, half * 4096:half * 4096 + n] = R[i]["yp"][:n]
    y_sample = np.stack([R[i]["ys"] for i in range(8)])
    odd = [2 * b + 1 for b in range(B)]
    conv_p = np.stack([R[i]["convp"] for i in odd], axis=1)
    conv_s = np.stack([R[i]["convs"] for i in range(8)], axis=1)
    k_p = np.stack([R[i]["kp"] for i in odd], axis=1).reshape(L, B, 512, 8, 64)
    v_p = np.stack([R[i]["vp"] for i in odd], axis=1).reshape(L, B, 512, 8, 64)
    k_s = np.stack([R[i]["ks"] for i in range(8)], axis=1).reshape(L, 8, TS, 8, 64)
    v_s = np.stack([R[i]["vs"] for i in range(8)], axis=1).reshape(L, 8, TS, 8, 64)
    pool_p = np.stack([R[i]["poolp"] for i in odd], axis=1)
    pool_s = np.stack([R[i]["pools"] for i in range(8)], axis=1)
    sgv_s = np.stack([R[i]["sgv"] for i in range(8)], axis=1)
    return (y_prompt, y_sample, conv_p, conv_s, k_p, v_p, k_s, v_s, pool_p, pool_s, sgv_s)
```

```python
import os
import numpy as np
import concourse.bass as bass
import concourse.mybir as mybir
from concourse.bass_utils import run_bass_kernel_spmd
from contextlib import ExitStack

F32 = mybir.dt.float32
F32R = mybir.dt.float32r
BF16 = mybir.dt.bfloat16
AF = mybir.ActivationFunctionType
ALU = mybir.AluOpType

L = 4
D = 2048
NCH = 16
DIN = 4096
DFF = 5632
NFF = 44
TP = 512
TS = 32
EPS = 1e-6
RING = 6
SLOT = 4096


class Buf:
    __slots__ = ("lw", "rd", "name", "excl")

    def __init__(self, name="", excl=False):
        self.lw = []
        self.rd = {}
        self.name = name
        self.excl = excl


class Rec:
    __slots__ = ("eng", "fn", "deps", "need", "val", "sem", "dma", "prev")


class Sched:
    ENG = ("pe", "act", "dve", "pool", "sp")

    def __init__(self, nc, nch=6):
        self.nc = nc
        self.q = {e: [] for e in self.ENG}
        self.nch = nch
        self.chcnt = {("sp", i): 0 for i in range(nch)}
        self.chcnt.update({("pool", i): 0 for i in range(nch)})
        self.chrr = {"sp": 0, "pool": 0}
        self.ndma = 0
        self.nowaw = False

    def _track(self, r, reads, writes):
        deps = []
        ex = [b for b in reads if b.excl]
        if ex:
            reads = [b for b in reads if not b.excl]
            writes = list(writes) + ex
        for b in reads:
            deps.extend(b.lw)
        for b in writes:
            deps.extend(b.lw)
            deps.extend(b.rd.values())
        for b in reads:
            if r.dma:
                self.ndma += 1
                b.rd[("dma", self.ndma)] = r
            else:
                b.rd[r.eng] = r
        for b in writes:
            b.lw = [r]
            b.rd = {}
        ds = []
        seen = set()
        for d in deps:
            if d is r or id(d) in seen:
                continue
            seen.add(id(d))
            if d.eng == "pe" and r.eng == "pe" and not d.dma and not r.dma:
                continue
            if self.nowaw and d.eng == r.eng and not d.dma and not r.dma:
                continue
            d.need = True
            ds.append(d)
        r.deps = ds

    def op(self, eng, fn, reads=(), writes=()):
        r = Rec()
        r.eng = eng
        r.fn = fn
        r.need = False
        r.val = None
        r.sem = eng
        r.dma = False
        r.prev = 0
        self._track(r, reads, writes)
        self.q[eng].append(r)
        return r

    def dma(self, queue, out, in_, reads=(), writes=(), **kw):
        r = Rec()
        r.eng = queue
        r.dma = True
        ch = self.chrr[queue]
        self.chrr[queue] = (ch + 1) % self.nch
        key = (queue, ch)
        r.prev = 16 * self.chcnt[key]
        self.chcnt[key] += 1
        r.val = 16 * self.chcnt[key]
        r.sem = key
        r.need = True
        r.fn = lambda e: e.dma_start(out=out, in_=in_, **kw)
        self._track(r, reads, writes)
        self.q[queue].append(r)
        return r

    def emit(self, es):
        nc = self.nc
        sems = {}
        for e in self.ENG:
            sems[e] = es.enter_context(nc.semaphore("s_" + e))
        for k in self.chcnt:
            sems[k] = es.enter_context(nc.semaphore("d_%s%d" % k))
        for e in self.ENG:
            c = 0
            for r in self.q[e]:
                if r.dma:
                    continue
                if r.need:
                    c += 1
                    r.val = c
        block = es.enter_context(nc.Block())
        engs = {"pe": block.tensor, "act": block.scalar, "dve": block.vector,
                "pool": block.gpsimd, "sp": block.sync}
        finals = dict((k, 16 * v) for k, v in self.chcnt.items())
        for e in self.ENG:
            recs = self.q[e]

            def body(eng, recs=recs, e=e):
                seen = {}
                for r in recs:
                    if r.dma and r.prev > 0 and seen.get(r.sem, 0) < r.prev:
                        eng.wait_ge(sems[r.sem], r.prev)
                        seen[r.sem] = r.prev
                    for d in r.deps:
                        if seen.get(d.sem, 0) < d.val:
                            eng.wait_ge(sems[d.sem], d.val)
                            seen[d.sem] = d.val
                    ins = r.fn(eng)
                    if r.dma:
                        ins.then_inc(sems[r.sem], 16)
                    elif r.need:
                        ins.then_inc(sems[r.sem], 1)
                if e == "sp":
                    for k, v in finals.items():
                        if v > 0:
                            eng.wait_ge(sems[k], v)

            engs[e](body)


def build_program(NH, NR, NL, do_sample):
    STOP = int(os.environ.get('KSTOP', '99'))
    SUB = int(os.environ.get('KSUB', '99'))
    KB = int(os.environ.get('KB', '255'))
    NT = NH + NR
    nc = bass.Bass("TRN2", target_bir_lowering=False)
    S = Sched(nc)
    es = ExitStack()

    def din(name, shape, dt=F32):
        return nc.dram_tensor(name, list(shape), dt, kind="ExternalInput").ap()

    def dout(name, shape):
        return nc.dram_tensor(name, list(shape), F32, kind="ExternalOutput").ap()

    xp = din("xp", [NT * TP, D])
    flag_d = din("flag", [128, 1])
    xs = din("xs", [TS, D])
    cp = din("cp", [16, 128])
    cs = din("cs", [16, 128])
    cconv = din("cconv", [NL, 30, 512])
    ck = din("ck", [NL, 512, 512])
    cv = din("cv", [NL, 512, 512])
    cpool = din("cpool", [NL, 15, 512])
    ada_w = din("ada_w", [NL, D, 6 * D])
    ada_b = din("ada_b", [NL, 96, 128])
    n1 = din("n1", [NL, 16, 128])
    n2 = din("n2", [NL, 16, 128])
    w_in = din("w_in", [NL, D, DIN])
    conv_w = din("conv_w", [NL, 31, 512])
    conv_b = din("conv_b", [NL, 4, 128])
    gn_g = din("gn_g", [NL, 4, 128])
    gn_b = din("gn_b", [NL, 4, 128])
    conv_pw = din("conv_pw", [NL, 512, 512])
    rel_bias = din("rel_bias", [NL, 8, 257])
    pool_w = din("pool_w", [NL, 4, 128, 128])
    pool_scale = din("pool_scale", [NL, 4, 128])
    sg_g = din("sg_g", [NL, 512])
    sg_bn = din("sg_bn", [NL, 512])
    sg_w = din("sg_w", [NL, 4, 128, 128])
    sg_b = din("sg_b", [NL, 1, 512])
    w_out = din("w_out", [NL, D, D])
    wg = din("wg", [NL, D, DFF])
    wu = din("wu", [NL, D, DFF])
    wd = din("wd", [NL, DFF, D])
    final_g = din("final_g", [16, 128])

    yp = dout("yp", [max(NR, 1) * TP, D])
    ys = dout("ys", [TS, D])
    convp = dout("convp", [L, 30, 512])
    convs = dout("convs", [L, 30, 512])
    kp = dout("kp", [L, 512, 512])
    vp = dout("vp", [L, 512, 512])
    ks = dout("ks", [L, TS, 512])
    vs = dout("vs", [L, TS, 512])
    poolp = dout("poolp", [L, 15, 512])
    pools = dout("pools", [L, 15, 512])
    sgv = dout("sgv", [L, TS, 512])

    Gd = nc.dram_tensor("Gd", [L * 8 * 767], F32, kind="Internal").ap()
    NWC = 92 * NL
    WCL = [nc.dram_tensor("WC%d" % l_, [92, 128, SLOT], BF16, kind="Internal").ap() for l_ in range(NL)]
    Khist = nc.dram_tensor("Khist", [L, 128, 2048], BF16, kind="Internal").ap()
    Vhist = nc.dram_tensor("Vhist", [L, 128, 2048], BF16, kind="Internal").ap()
    bG = [Buf("G%d" % l) for l in range(L)]
    bKh = [Buf() for _ in range(L)]
    bVh = [Buf() for _ in range(L)]

    def sb(name, shape, dt):
        return es.enter_context(nc.sbuf_tensor(name, list(shape), dt))

    ident_f = sb("ident_f", [128, 128], F32)
    ident_b = sb("ident_b", [128, 128], BF16)
    ones_f = sb("ones_f", [128, 128], F32)
    ones_r = sb("ones_r", [128, 128], F32R)
    blk_b = sb("blk_b", [128, 128], BF16)
    ones_b = sb("ones_b", [128, 128], BF16)
    tril = sb("tril", [128, 128], F32)
    antiI = sb("antiI", [128, 128], F32)
    flag = sb("flagt", [128, 1], F32)
    invc = sb("invc", [128, 4, 16], F32)
    invd = sb("invd", [128, 4, 16], F32)
    AD = sb("AD", [128, L, 6, 16, 2], F32)
    modl = sb("modl", [128, 96, 2], F32)
    adabT = sb("adabT", [128, 96], F32)
    n1T = sb("n1T", [128, L, 16], F32)
    n2T = sb("n2T", [128, L, 16], F32)
    fgT = sb("fgT", [128, 16], F32)
    cbT = sb("cbT", [128, L, 4], F32)
    ggT = sb("ggT", [128, L, 4], F32)
    gbT = sb("gbT", [128, L, 4], F32)
    pscT = sb("pscT", [128, L, 4], F32)
    cwT = sb("cwT", [128, L, 4, 31], F32)
    WT = sb("WT", [128, L, 4, 128], BF16)
    b2f = sb("b2f", [1, 512], F32)
    b2r = sb("b2r", [1, 512], F32R)
    csf = sb("csf", [128, 16, 2], F32)
    csb = sb("csb", [128, 16, 2], BF16)
    hconv = sb("hconv", [128, L, 4, 30], BF16)
    hpool = sb("hpool", [128, L, 4, 15], F32)
    stg = sb("stg", [128, 2, 128], F32)
    st6 = sb("st6", [128, 6], F32)
    mv = sb("mv", [128, 2], F32)
    rv = sb("rv", [128, 2], F32)
    t16 = sb("t16", [128, 16], F32)
    u32 = sb("u32", [128, 4, 32], F32)
    ostage = sb("ostage", [128, 512], F32)

    b_const = Buf("const")
    b_par = Buf("par")
    b_stg = [Buf(), Buf()]
    b_hconv = [Buf() for _ in range(L)]
    b_hpool = [Buf() for _ in range(L)]
    b_ostage = Buf()
    b_u32 = Buf()
    b_small = Buf()
    b_b2 = Buf()
    b_ybf = Buf()

    xT = sb("xT", [128, 16, TP], F32)
    hT = sb("hT", [128, 16, TP], BF16)
    mixT = sb("mixT", [128, 16, TP], BF16)
    ring = sb("ring", [128, RING, SLOT], BF16)
    sq = sb("sq", [128, 2, TP], BF16)
    ybf = sb("ybf", [128, TP], BF16)
    rstd = sb("rstd", [128, TP], F32)
    sd = sb("sd", [128, TP], F32)
    tmp = sb("tmp", [128, 2, TP], F32)
    uext = sb("uext", [128, 4, 30 + TP], BF16)
    sl = sb("sl", [128, 2, TP], BF16)
    RA = sb("RA", [128, 2048], F32)
    RB = sb("RB", [128, 5120], BF16)
    RC = sb("RC", [128, 1280], F32)
    RD = sb("RD", [128, 4096], BF16)
    RE = sb("RE", [128, 4096], BF16)
    RF = sb("RF", [128, 2048], BF16)
    RG = sb("RG", [128, 3, 512], F32)

    bx = [Buf("x%d" % c) for c in range(16)]
    bh = [Buf("h%d" % c) for c in range(16)]
    bmix = [Buf("m%d" % c) for c in range(16)]
    bring = [Buf("ring%d" % i) for i in range(RING)]
    bsq = [Buf(), Buf()]
    brstd = Buf()
    bsd = Buf()
    btmp = [Buf(), Buf()]
    buext = [Buf() for _ in range(4)]
    bsl = [Buf(), Buf()]
    bRA = Buf("RA")
    bRB = [Buf() for _ in range(8)]
    bRC = [Buf(), Buf()]
    EXTB = [[bRB[0], bRB[1]], [bRB[1], bRB[2], bRB[3]], [bRB[3], bRB[4]], [bRB[4], bRB[5], bRB[6]]]
    PTMPB = [[bRC[0]], [bRC[0], bRC[1]]]
    bRD = [Buf() for _ in range(4)]
    bRE = [Buf() for _ in range(8)]
    bRF = [Buf() for _ in range(4)]
    RG0 = [Buf(), Buf()]
    RG1 = [Buf(), Buf()]
    RG2 = [Buf(), Buf()]

    xtok = RA
    diag = RA[:, 0:1984].bitcast(BF16).rearrange("p (k m) -> p k m", m=128)
    EB = RB[:].rearrange("p (h j) -> p h j", j=640)
    ext = RB[:, 0:4 * 2 * (15 + TP)].bitcast(F32).rearrange("p (g j) -> p g j", j=15 + TP)
    EBraw = RC[:].rearrange("p (a j) -> p a j", j=640)
    ptmp = RC[:, 0:2 * (15 + TP)].rearrange("p (a j) -> p a j", j=15 + TP)
    Kb = RD[:].rearrange("p (c j) -> p c j", j=1024)
    vtok = RD[:].bitcast(F32).rearrange("p (n j) -> p n j", j=512)
    Vb = RE[:].rearrange("p (n j) -> p n j", j=512)
    hid = RE[:].rearrange("p (a c j) -> p a c j", a=2, j=TP)
    actb = RF[:].rearrange("p (c j) -> p c j", j=TP)
    Qb = actb
    mb = actb
    vtb = actb
    ysb = RG[:, 0, :]
    dsb = RG[:, 1, :]
    nsb = RG[:, 2, :]
    ebuf = RG[:, 0, :].bitcast(BF16).rearrange("p (a j) -> p a j", j=TP)
    PTb = RG[:, 1, :].bitcast(BF16).rearrange("p (a j) -> p a j", j=TP)
    rdb = RG[:, 2, :]
    Gbb = RG[:, 0, :]
    Bbb = RG[:, 1, :]
    tsb = RG[:, 2, :]

    psum = es.enter_context(nc.psum_tensor("psum", [128, 8, 512], F32))
    bps = [Buf("ps%d" % i, excl=True) for i in range(8)]
    held = set()
    psrr = [0]

    def ps():
        for _ in range(16):
            i = psrr[0]
            psrr[0] = (i + 1) % 8
            if i not in held:
                return i
        raise RuntimeError("no psum bank")

    ringrr = [0]

    def mm(out, lhsT, rhs, start, stop, R, W, **kw):
        S.op("pe", lambda e: e.matmul(out, lhsT, rhs, start=start, stop=stop, **kw), R, W)

    def tr(out, in_, ident, R, W):
        S.op("pe", lambda e: e.transpose(out, in_, ident), R, W)

    def act(out, in_, func, R, W, bias=None, scale=None):
        kw = {}
        if bias is not None:
            kw["bias"] = bias
        if scale is not None:
            kw["scale"] = scale
        S.op("act", lambda e: e.activation(out, in_, func, **kw), R, W)

    def tt(out, in0, in1, op, R, W, eng="dve"):
        S.op(eng, lambda e: e.tensor_tensor(out, in0, in1, op), R, W)

    def ts(out, in0, s1, s2, op0, op1, R, W, eng="dve"):
        if op1 is None:
            S.op(eng, lambda e: e.tensor_scalar(out, in0, s1, None, op0), R, W)
        else:
            S.op(eng, lambda e: e.tensor_scalar(out, in0, s1, s2, op0, op1), R, W)

    def stt(out, in0, scalar, in1, op0, op1, R, W):
        S.op("dve", lambda e: e.scalar_tensor_tensor(out, in0, scalar, in1, op0, op1), R, W)

    def cp_(out, in_, R, W, eng="dve"):
        if eng == "act":
            act(out, in_, AF.Copy, R, W)
        else:
            S.op(eng, lambda e: e.tensor_copy(out, in_), R, W)

    def recip(out, in_, R, W):
        S.op("dve", lambda e: e.reciprocal(out, in_), R, W)

    def memset(ap, val, W, eng="dve"):
        S.op(eng, lambda e: e.memset(ap, val), (), W)

    wcache = {}

    def wslab(src, shape_free, key=None):
        i = ringrr[0]
        ringrr[0] = (i + 1) % RING
        n = 1
        for s_ in shape_free:
            n *= s_
        flat = ring[:, i, 0:n]
        v = flat
        if len(shape_free) == 2:
            v = v.rearrange("p (a b) -> p a b", b=shape_free[1])
        if key is not None and key in wcache:
            idx, cb = wcache[key]
            S.dma("sp", flat, WCL[idx // 92][idx % 92, :, 0:n], [cb], [bring[i]])
        else:
            S.dma("pool", v, src, (), [bring[i]])
            if key is not None and len(wcache) < NWC:
                idx = len(wcache)
                cb = Buf()
                wcache[key] = (idx, cb)
                S.dma("sp", WCL[idx // 92][idx % 92, :, 0:n], flat, [bring[i]], [cb])
        return v, bring[i]

    memset(ones_f[:], 1.0, [b_const])
    memset(ones_b[:], 1.0, [b_const])
    memset(tril[:], 0.0, [b_const])
    memset(tril[0:64, 0:64], 1.0 / 64, [b_const])
    memset(tril[64:128, 64:128], 1.0 / 64, [b_const])
    cp_(blk_b[:], tril[:], [b_const], [b_const])
    cp_(ones_r[:], ones_f[:], [b_const], [b_const])
    S.op("pool", lambda e: e.affine_select(out=ident_f[:], in_=ones_f[:], pattern=[[-1, 128]],
                                           compare_op=ALU.is_equal, fill=0.0, base=0,
                                           channel_multiplier=1), [b_const], [b_const])
    S.op("pool", lambda e: e.affine_select(out=antiI[:], in_=ones_f[:], pattern=[[1, 128]],
                                           compare_op=ALU.is_equal, fill=0.0, base=-127,
                                           channel_multiplier=1), [b_const], [b_const])
    S.op("pool", lambda e: e.affine_select(out=tril[:], in_=ones_f[:], pattern=[[-1, 128]],
                                           compare_op=ALU.is_ge, fill=0.0, base=0,
                                           channel_multiplier=1), [b_const], [b_const])
    cp_(ident_b[:], ident_f[:], [b_const], [b_const])
    memset(hconv[:], 0.0, b_hconv)
    memset(hpool[:], 0.0, b_hpool)
    S.dma("sp", flag[:], flag_d, (), [b_const])
    for g in range(4):
        w = 2 ** (g + 1)
        for t in range(16):
            c = 1.0 / min(w, t + 1)
            memset(invc[:, g, t:t + 1], c, [b_const])
            memset(invd[:, g, t:t + 1], 1.0 / w - c, [b_const])
    stt(invc[:], invd[:], flag[:, 0:1], invc[:], ALU.mult, ALU.add, [b_const], [b_const])

    stgrr = [0]

    def load_fm(dst, src_rows, R):
        i = stgrr[0]
        stgrr[0] = 1 - i
        S.dma("sp", stg[0:R, i, :], src_rows, (), [b_stg[i]])
        b = ps()
        tr(psum[:, b, 0:R], stg[0:R, i, :], ident_f[0:R, 0:R], [b_stg[i], b_const], [bps[b]])
        cp_(dst, psum[:, b, 0:R], [bps[b]], [b_par], eng="act")

    load_fm(fgT[:], final_g, 16)
    load_fm(csf[:, :, 0], cp, 16)
    load_fm(csf[:, :, 1], cs, 16)
    act(csb[:], csf[:], AF.Silu, [b_par], [b_par])
    for l in range(NL):
        load_fm(n1T[:, l, :], n1[l], 16)
        load_fm(n2T[:, l, :], n2[l], 16)
        load_fm(cbT[:, l, :], conv_b[l], 4)
        load_fm(ggT[:, l, :], gn_g[l], 4)
        load_fm(gbT[:, l, :], gn_b[l], 4)
        load_fm(pscT[:, l, :], pool_scale[l], 4)
        for c in range(4):
            load_fm(cwT[:, l, c, :], conv_w[l][:, c * 128:(c + 1) * 128], 31)
        for g in range(4):
            i = stgrr[0]
            stgrr[0] = 1 - i
            S.dma("sp", stg[:, i, :], sg_w[l, g], (), [b_stg[i]])
            tt(stg[:, i, :], stg[:, i, :], tril[:], ALU.mult, [b_stg[i], b_const], [b_stg[i]])
            b = ps()
            tr(psum[:, b, 0:128], stg[:, i, :], ident_f[:], [b_stg[i], b_const], [bps[b]])
            cp_(WT[:, l, g, :], psum[:, b, 0:128], [bps[b]], [b_par], eng="act")
        rbs = RA[0:8, 0:257]
        Gsb = RA[0:8, 512:1279]
        S.dma("sp", rbs, rel_bias[l], (), [bRA])
        memset(Gsb, 0.0, [bRA])
        ts(Gsb[:, 0:511], Gsb[:, 0:511], rbs[:, 0:1], None, ALU.add, None, [bRA], [bRA])
        cp_(Gsb[:, 511:767], rbs[:, 0:256], [bRA], [bRA])
        S.dma("sp", Gd[l * 8 * 767:(l + 1) * 8 * 767].rearrange("(h t) -> h t", t=767), Gsb,
              [bRA], [bG[l]])

    for l in range(NL):
        load_fm(adabT[:], ada_b[l], 96)
        bm = ps()
        held.add(bm)
        awv = ada_w[l].rearrange("(c p) m -> p c m", p=128)
        for si in range(48):
            slot, bs = wslab(awv[:, :, si * 256:(si + 1) * 256], [16, 256])
            for mc in range(2):
                m = si * 2 + mc
                for k in range(16):
                    mm(psum[:, bm, 2 * m:2 * m + 2], slot[:, k, mc * 128:(mc + 1) * 128], csb[:, k, :],
                       k == 0, k == 15, [bs, b_par], [bps[bm]], skip_group_check=True)
        pm = psum[:, bm, 0:192].rearrange("p (m s) -> p m s", s=2)
        for s_ in range(2):
            tt(modl[:, :, s_], pm[:, :, s_], adabT[:], ALU.add, [bps[bm], b_par], [b_par])
        held.discard(bm)
        for s_ in range(2):
            stt(AD[:, l, 0, :, s_], modl[:, 16:32, s_], 1.0, n1T[:, l, :], ALU.add, ALU.mult, [b_par], [b_par])
            cp_(AD[:, l, 1, :, s_], modl[:, 0:16, s_], [b_par], [b_par])
            cp_(AD[:, l, 2, :, s_], modl[:, 32:48, s_], [b_par], [b_par])
            stt(AD[:, l, 3, :, s_], modl[:, 64:80, s_], 1.0, n2T[:, l, :], ALU.add, ALU.mult, [b_par], [b_par])
            cp_(AD[:, l, 4, :, s_], modl[:, 48:64, s_], [b_par], [b_par])
            cp_(AD[:, l, 5, :, s_], modl[:, 80:96, s_], [b_par], [b_par])

    evrr = [0]

    def evac_eng():
        evrr[0] ^= 1
        return "act" if evrr[0] else "dve"

    def load_x(src_rows, T):
        nb = max(T // 128, 1)
        rows = min(T, 128)
        for tb in range(nb):
            S.dma("sp", xtok[0:rows, :], src_rows[tb * 128:tb * 128 + rows, :], (), [bRA])
            for cg in range(4):
                b = ps()
                for j in range(4):
                    c = cg * 4 + j
                    tr(psum[:, b, j * rows:(j + 1) * rows], xtok[0:rows, c * 128:(c + 1) * 128],
                       ident_f[0:rows, 0:rows], [bRA, b_const], [bps[b]])
                cp_(xT[:, cg * 4:cg * 4 + 4, tb * 128:tb * 128 + rows],
                    psum[:, b, 0:4 * rows].rearrange("p (j t) -> p j t", t=rows),
                    [bps[b]], bx[cg * 4:cg * 4 + 4], eng=evac_eng())

    def rms_stats(T):
        b = ps()
        for c in range(16):
            act(sq[:, c % 2, 0:T], xT[:, c, 0:T], AF.Square, [bx[c]], [bsq[c % 2]])
            mm(psum[:, b, 0:T], ones_b[:], sq[:, c % 2, 0:T], c == 0, c == 15, [bsq[c % 2], b_const], [bps[b]])
        act(sd[:, 0:T], psum[:, b, 0:T], AF.Sqrt, [bps[b]], [bsd], bias=EPS_AP[:, 0:1], scale=1.0 / D)
        recip(rstd[:, 0:T], sd[:, 0:T], [bsd], [brstd])

    def norm_to_h(T, a_ap, b_ap):
        rms_stats(T)
        for c in range(16):
            stt(tmp[:, c % 2, 0:T], xT[:, c, 0:T], a_ap[:, c:c + 1], rstd[:, 0:T], ALU.mult, ALU.mult,
                [bx[c], brstd, b_par], [btmp[c % 2]])
            act(hT[:, c, 0:T], tmp[:, c % 2, 0:T], AF.Identity, [btmp[c % 2], b_par], [bh[c]],
                bias=b_ap[:, c:c + 1], scale=1.0)

    def proj_fm(wl, col0, nchunk, T, consume, wkey=None):
        wv = wl.rearrange("(c p) m -> p c m", p=128)
        for sj in range(nchunk // 2):
            slot, bs = wslab(wv[:, :, col0 + sj * 256:col0 + (sj + 1) * 256], [16, 256], key=(wkey, col0 + sj * 256))
            banks = []
            for mc in range(2):
                b = ps()
                for k in range(16):
                    mm(psum[:, b, 0:T], slot[:, k, mc * 128:(mc + 1) * 128], hT[:, k, 0:T], k == 0, k == 15,
                       [bs, bh[k]], [bps[b]])
                banks.append(b)
            for mc in range(2):
                consume(sj * 2 + mc, banks[mc])

    def proj_tm(wl, col0, T, consume, wkey=None):
        wv = wl.rearrange("(c p) m -> p c m", p=128)
        nb = max(T // 128, 1)
        rows = min(T, 128)
        banks = [ps() for _ in range(nb)]
        for i_ in banks:
            held.add(i_)
        for half in range(2):
            slot, bs = wslab(wv[:, :, col0 + half * 256:col0 + (half + 1) * 256], [16, 256], key=(wkey, col0 + half * 256))
            for tb in range(nb):
                b = banks[tb]
                for k in range(16):
                    mm(psum[0:rows, b, half * 256:(half + 1) * 256], hT[:, k, tb * 128:tb * 128 + rows],
                       slot[:, k, :], k == 0, k == 15, [bs, bh[k]], [bps[b]])
        for tb in range(nb):
            consume(tb, banks[tb])
            held.discard(banks[tb])

    def out_rows_fm(dst, src_fm, n, R):
        b = ps()
        tr(psum[0:n, b, 0:128], src_fm, ident_f[:], R + [b_const], [bps[b]])
        cp_(ostage[0:n, 0:128], psum[0:n, b, 0:128], [bps[b]], [b_ostage], eng="act")
        S.dma("sp", dst, ostage[0:n, 0:128], [b_ostage], ())

    def layer(l, T, sidx, mode, first, halo_save, first_real, want_state, O):
        sample = sidx == 1
        a1 = AD[:, l, 0, :, sidx]
        b1 = AD[:, l, 1, :, sidx]
        g1 = AD[:, l, 2, :, sidx]
        a2 = AD[:, l, 3, :, sidx]
        b2_ = AD[:, l, 4, :, sidx]
        g2 = AD[:, l, 5, :, sidx]
        wl = w_in[l]
        full = mode == "full"
        if STOP <= 0:
            return
        norm_to_h(T, a1, b1)
        if STOP <= 1:
            return
        if full and SUB >= 2 and STOP > 2:
            for hd in range(8):
                src = bass.AP(tensor=Gd.tensor, offset=(l * 8 + hd) * 767, ap=[[1, 128], [128, 5], [1, 128]])
                S.dma("sp", EBraw[:, hd % 2, :].rearrange("p (a k) -> p a k", k=128), src, [bG[l]], [bRC[hd % 2]])
                b = ps()
                for jb in range(4):
                    mm(psum[:, b, jb * 128:(jb + 1) * 128], EBraw[:, hd % 2, (4 - jb) * 128:(5 - jb) * 128], antiI[:], True, True,
                       [bRC[hd % 2], b_const], [bps[b]])
                act(EB[:, hd, 0:512], psum[:, b, :], AF.Exp, [bps[b]], [bRB[hd]])
                b = ps()
                mm(psum[:, b, 0:128], EBraw[:, hd % 2, 0:128], antiI[:], True, True, [bRC[hd % 2], b_const], [bps[b]])
                act(EB[:, hd, 512:640], psum[:, b, 0:128], AF.Exp, [bps[b]], [bRB[hd]])
            memset(EB[0:64, :, 576:640], 0.0, bRB)
            memset(EB[64:128, :, 0:64], 0.0, bRB)

        if sample:
            S.dma("sp", xtok[0:30, 0:512], cconv[l], (), [bRA])
            for c in range(4):
                b = ps()
                tr(psum[:, b, 0:30], xtok[0:30, c * 128:(c + 1) * 128], ident_f[0:30, 0:30], [bRA, b_const], [bps[b]])
                cp_(uext[:, c, 0:30], psum[:, b, 0:30], [bps[b]], [buext[c]], eng="act")
        else:
            for c in range(4):
                cp_(uext[:, c, 0:30], hconv[:, l, c, :], [b_hconv[l]], [buext[c]])
        zab = {}

        def cons_a(j, b):
            zab[j] = b

        proj_fm(wl, 0, 4, T, cons_a, wkey=("win", l))
        for j in range(4):
            held.add(zab[j])

        def cons_g(j, b):
            act(tmp[:, j % 2, 0:T], psum[:, b, 0:T], AF.Sigmoid, [bps[b]], [btmp[j % 2]])
            tt(uext[:, j, 30:30 + T], psum[:, zab[j], 0:T], tmp[:, j % 2, 0:T], ALU.mult,
               [bps[zab[j]], btmp[j % 2]], [buext[j]])
            if want_state:
                tt(u32[:, j, 0:30], psum[:, zab[j], T - 30:T], tmp[:, j % 2, T - 30:T], ALU.mult,
                   [bps[zab[j]], btmp[j % 2]], [b_u32])
            held.discard(zab[j])

        proj_fm(wl, 512, 4, T, cons_g, wkey=("win", l))
        if want_state:
            for c in range(4):
                out_rows_fm(O["conv"][l][:, c * 128:(c + 1) * 128], u32[:, c, 0:30], 30, [b_u32])
        if not sample:
            for c in range(4):
                if halo_save:
                    ts(hconv[:, l, c, :], uext[:, c, T:T + 30], flag[:, 0:1], None, ALU.mult, None,
                       [buext[c], b_const], [b_hconv[l]])
                else:
                    cp_(hconv[:, l, c, :], uext[:, c, T:T + 30], [buext[c]], [b_hconv[l]])
        if full:
            for c in range(4):
                for k in range(31):
                    S.nowaw = k > 0
                    ts(diag[:, k, :], ident_b[:], cwT[:, l, c, k:k + 1], None, ALU.mult, None,
                       [b_const, b_par], [bRA])
                S.nowaw = False
                b = ps()
                for k in range(31):
                    mm(psum[:, b, 0:T], diag[:, k, :], uext[:, c, k:k + T], k == 0, k == 30,
                       [bRA, buext[c]], [bps[b]])
                act(ysb[:, 0:T], psum[:, b, 0:T], AF.Identity, [bps[b], b_par], RG0,
                    bias=cbT[:, l, c:c + 1], scale=1.0)
                act(ybf[:, 0:T], psum[:, b, 0:T], AF.Identity, [bps[b], b_par], [b_ybf],
                    bias=cbT[:, l, c:c + 1], scale=1.0)
                b2 = ps()
                mm(psum[:, b2, 0:T], blk_b[:], ybf[:, 0:T], True, True, [b_ybf, b_const], [bps[b2]])
                tt(dsb[:, 0:T], RG[:, 0, 0:T], psum[:, b2, 0:T], ALU.subtract, RG0 + [bps[b2]], RG1)
                act(sq[:, 0, 0:T], dsb[:, 0:T], AF.Square, RG1, [bsq[0]])
                b3 = ps()
                mm(psum[:, b3, 0:T], blk_b[:], sq[:, 0, 0:T], True, True, [bsq[0], b_const], [bps[b3]])
                act(sd[:, 0:T], psum[:, b3, 0:T], AF.Sqrt, [bps[b3]], [bsd], bias=EPS_AP[:, 0:1], scale=1.0)
                recip(nsb[:, 0:T], sd[:, 0:T], [bsd], RG2)
                tt(dsb[:, 0:T], dsb[:, 0:T], nsb[:, 0:T], ALU.mult, RG1 + RG2, RG1)
                act(actb[:, c, 0:T], dsb[:, 0:T], AF.Silu, RG1 + [b_par], [bRF[c]],
                    bias=gbT[:, l, c:c + 1], scale=ggT[:, l, c:c + 1])
            slot, bs = wslab(conv_pw[l].rearrange("(c p) m -> p c m", p=128), [4, 512], key=("pw", l))
            for m in range(4):
                b = ps()
                for kc in range(4):
                    mm(psum[:, b, 0:T], slot[:, kc, m * 128:(m + 1) * 128], actb[:, kc, 0:T], kc == 0, kc == 3,
                       [bs, bRF[kc]], [bps[b]])
                cp_(mixT[:, m, 0:T], psum[:, b, 0:T], [bps[b]], [bmix[m]], eng=evac_eng())

        if STOP <= 2:
            return
        H = 512
        if sample:
            S.dma("sp", xtok[:].rearrange("p (n f) -> p n f", f=512),
                  ck[l].rearrange("(n p) f -> p n f", p=128), (), [bRA])
            for c in range(4):
                b = ps()
                for kb in range(4):
                    tr(psum[:, b, kb * 128:(kb + 1) * 128], xtok[:, kb * 512 + c * 128:kb * 512 + (c + 1) * 128],
                       ident_f[:], [bRA, b_const], [bps[b]])
                cp_(Kb[:, c, 0:512], psum[:, b, :], [bps[b]], [bRD[c]], eng=evac_eng())
            S.dma("sp", xtok[:].rearrange("p (n f) -> p n f", f=512),
                  cv[l].rearrange("(n p) f -> p n f", p=128), (), [bRA])
            cp_(Vb[:, 0:4, :], xtok[:].rearrange("p (n f) -> p n f", f=512), [bRA], bRE[0:4])
        elif not first:
            S.dma("sp", Kb[:, :, 0:512], Khist[l].rearrange("p (c j) -> p c j", j=512), [bKh[l]], bRD)
            S.dma("sp", Vb[:, 0:4, :], Vhist[l].rearrange("p (n j) -> p n j", j=512), [bVh[l]], bRE[0:4])
        if full and (KB & 1):
            def cons_q(j, b):
                act(Qb[:, j, 0:T], psum[:, b, 0:T], AF.Copy, [bps[b]], [bRF[j]], scale=0.125)

            proj_fm(wl, 1024, 4, T, cons_q, wkey=("win", l))

        def cons_k(j, b):
            cp_(Kb[:, j, H:H + T], psum[:, b, 0:T], [bps[b]], [bRD[j]])
            if want_state and (KB & 4):
                cp_(tmp[:, j % 2, 0:T], psum[:, b, 0:T], [bps[b]], [btmp[j % 2]], eng="act")
                nb = max(T // 128, 1)
                rows = min(T, 128)
                b4 = ps()
                for tb in range(nb):
                    tr(psum[0:rows, b4, tb * 128:(tb + 1) * 128], tmp[:, j % 2, tb * 128:tb * 128 + rows],
                       ident_f[:], [btmp[j % 2], b_const], [bps[b4]])
                cp_(ostage[0:rows, 0:nb * 128], psum[0:rows, b4, 0:nb * 128], [bps[b4]], [b_ostage], eng="act")
                if not (KB & 64):
                    pass
                elif sample:
                    S.dma("sp", O["k"][l][:, j * 128:(j + 1) * 128], ostage[0:rows, 0:128], [b_ostage], ())
                else:
                    S.dma("sp", O["k"][l][:, j * 128:(j + 1) * 128].rearrange("(tb p) f -> p tb f", p=128),
                          ostage[:, :].rearrange("p (tb f) -> p tb f", f=128), [b_ostage], ())

        if KB & 2:
            proj_fm(wl, 1536, 4, T, cons_k, wkey=("win", l))
        rowsT = min(T, 128)

        def cons_v(tb, b):
            cp_(Vb[0:rowsT, 4 + tb, :], psum[0:rowsT, b, :], [bps[b]], [bRE[4 + tb]])
            if want_state and (KB & 16):
                cp_(ostage[0:rowsT, :], psum[0:rowsT, b, :], [bps[b]], [b_ostage], eng="act")
                S.dma("sp", O["v"][l][tb * 128:tb * 128 + rowsT, :], ostage[0:rowsT, :], [b_ostage], ())

        if KB & 8:
            proj_tm(wl, 2048, T, cons_v, wkey=("win", l))
        if not sample and (KB & 32):
            S.dma("sp", Khist[l].rearrange("p (c j) -> p c j", j=512), Kb[:, :, H:H + T], bRD, [bKh[l]])
            S.dma("sp", Vhist[l].rearrange("p (n j) -> p n j", j=512), Vb[:, 4:8, :], bRE[4:8], [bVh[l]])

        if full and SUB >= 2:
            blist = []
            if sample:
                for b_ in range(4):
                    blist.append((b_ * 128, 128, b_, 8 - 2 * b_, 0, T, False))
                blist.append((512, T, 4, 0, 0, T, False))
            else:
                for b_ in range(8):
                    if first and b_ < 4:
                        continue
                    n0 = max(0, 2 * b_ - 8)
                    n1_ = min(7, 2 * b_ + 1)
                    blist.append((b_ * 128, 128, b_, n0 - 2 * b_ + 8, 64 * n0, 64 * (n1_ - n0 + 1),
                                  first_real and b_ < 4))
            jj = 0
            for hp in range(4 if SUB >= 3 else 0):
                bo = ps()
                held.add(bo)
                bd = ps()
                held.add(bd)
                items = [(s_, bi) for s_ in range(2) for bi in range(len(blist))]

                def stage1(s_, bi):
                    nonlocal jj
                    kc0, nk, vb_, i0, q0, N, uf = blist[bi]
                    hd = 2 * hp + s_
                    pr = slice(64 * s_, 64 * s_ + 64)
                    b = ps()
                    mm(psum[0:nk, b, 0:N], Kb[pr, hp, kc0:kc0 + nk], Qb[pr, hp, q0:q0 + N], True, True,
                       [bRD[hp], bRF[hp]], [bps[b]])
                    e_ = jj % 2
                    jj += 1
                    act(ebuf[0:nk, e_, 0:N], psum[0:nk, b, 0:N], AF.Exp, [bps[b]], [RG0[e_]])
                    if uf:
                        stt(PTb[0:nk, e_, 0:N], ebuf[0:nk, e_, 0:N], flag[0:nk, 0:1],
                            EB[0:nk, hd, 64 * i0:64 * i0 + N], ALU.mult, ALU.mult,
                            [RG0[e_], bRB[hd], b_const], [RG1[e_]])
                    else:
                        tt(PTb[0:nk, e_, 0:N], ebuf[0:nk, e_, 0:N], EB[0:nk, hd, 64 * i0:64 * i0 + N], ALU.mult,
                           [RG0[e_], bRB[hd]], [RG1[e_]])
                    return (s_, bi, e_)

                def stage2(s_, bi, e_):
                    kc0, nk, vb_, i0, q0, N, uf = blist[bi]
                    hd = 2 * hp + s_
                    pr = slice(64 * s_, 64 * s_ + 64)
                    fst = bi == 0
                    lst = bi == len(blist) - 1
                    mm(psum[pr, bo, q0:q0 + N], Vb[0:nk, vb_, hd * 64:(hd + 1) * 64], PTb[0:nk, e_, 0:N],
                       fst, lst, [bRE[vb_], RG1[e_]], [bps[bo]], skip_group_check=True)
                    mm(psum[pr, bd, q0:q0 + N], ones_b[0:nk, 0:64], PTb[0:nk, e_, 0:N],
                       fst, lst, [b_const, RG1[e_]], [bps[bd]], skip_group_check=True)

                prev = None
                for (s_, bi) in items:
                    cur = stage1(s_, bi)
                    if prev is not None:
                        stage2(*prev)
                    prev = cur
                stage2(*prev)
                recip(rdb[:, 0:T], psum[:, bd, 0:T], [bps[bd]], RG2)
                tt(mixT[:, 4 + hp, 0:T], psum[:, bo, 0:T], rdb[:, 0:T], ALU.mult, [bps[bo]] + RG2, [bmix[4 + hp]])
                held.discard(bo)
                held.discard(bd)

        if STOP <= 3:
            return
        if sample:
            S.dma("sp", xtok[0:15, 0:512], cpool[l], (), [bRA])
            for g in range(4):
                b = ps()
                tr(psum[:, b, 0:15], xtok[0:15, g * 128:(g + 1) * 128], ident_f[0:15, 0:15], [bRA, b_const], [bps[b]])
                cp_(ext[:, g, 0:15], psum[:, b, 0:15], [bps[b]], EXTB[g], eng="act")
        else:
            for g in range(4):
                cp_(ext[:, g, 0:15], hpool[:, l, g, :], [b_hpool[l]], EXTB[g])

        def cons_p(j, b):
            cp_(ext[:, j, 15:15 + T], psum[:, b, 0:T], [bps[b]], EXTB[j], eng="act")

        proj_fm(wl, 2560, 4, T, cons_p, wkey=("win", l))
        if not sample:
            for g in range(4):
                if halo_save:
                    ts(hpool[:, l, g, :], ext[:, g, T:T + 15], flag[:, 0:1], None, ALU.mult, None,
                       EXTB[g] + [b_const], [b_hpool[l]])
                else:
                    cp_(hpool[:, l, g, :], ext[:, g, T:T + 15], EXTB[g], [b_hpool[l]])
        if want_state:
            for g in range(4):
                out_rows_fm(O["pool"][l][:, g * 128:(g + 1) * 128], ext[:, g, T:T + 15], 15, EXTB[g])
        if full:
            E = 15 + T
            slot, bs = wslab(pool_w[l].rearrange("g c d -> c g d"), [4, 128], key=("poolw", l))
            for g in range(4):
                e_ = ext[:, g, :]
                t1 = ptmp[:, 0, :]
                t2 = ptmp[:, 1, :]
                tt(t1[:, 1:E], e_[:, 1:E], e_[:, 0:E - 1], ALU.add, EXTB[g], PTMPB[0])
                win = t1
                wb = PTMPB[0]
                if g >= 1:
                    tt(t2[:, 3:E], t1[:, 3:E], t1[:, 1:E - 2], ALU.add, PTMPB[0], PTMPB[1])
                    win = t2
                    wb = PTMPB[1]
                if g >= 2:
                    tt(t1[:, 7:E], t2[:, 7:E], t2[:, 3:E - 4], ALU.add, PTMPB[1], PTMPB[0])
                    win = t1
                    wb = PTMPB[0]
                if g >= 3:
                    tt(t2[:, 15:E], t1[:, 15:E], t1[:, 7:E - 8], ALU.add, PTMPB[0], PTMPB[1])
                    win = t2
                    wb = PTMPB[1]
                w_ = 2 ** (g + 1)
                stt(mb[:, g, 0:T], win[:, 15:E], 1.0 / w_, e_[:, 15:E], ALU.mult, ALU.subtract,
                    wb + EXTB[g], [bRF[g]])
                if first_real:
                    tt(t16[:, :], win[:, 15:31], invc[:, g, :], ALU.mult, wb + [b_const], [b_small])
                    tt(mb[:, g, 0:16], t16[:, :], e_[:, 15:31], ALU.subtract, [b_small] + EXTB[g], [bRF[g]])
                b = ps()
                mm(psum[:, b, 0:T], slot[:, g, :], mb[:, g, 0:T], True, True, [bs, bRF[g]], [bps[b]])
                act(mixT[:, 8 + g, 0:T], psum[:, b, 0:T], AF.Copy, [bps[b], b_par], [bmix[8 + g]],
                    scale=pscT[:, l, g:g + 1])

        if STOP <= 4:
            return
        if full:
            S.dma("sp", b2f[:], sg_b[l], (), [b_b2])
            cp_(b2r[:], b2f[:], [b_b2], [b_b2])
            S.dma("sp", Gbb[:, :], sg_g[l].partition_broadcast(128), (), RG0)
            S.dma("sp", Bbb[:, :], sg_bn[l].partition_broadcast(128), (), RG1)
            nb = max(T // 128, 1)

            def cons_s(tb, b):
                S.op("dve", lambda e: e.bn_stats(st6[0:rowsT, :], psum[0:rowsT, b, :]), [bps[b]], [b_small])
                S.op("dve", lambda e: e.bn_aggr(mv[0:rowsT, :], st6[0:rowsT, :]), [b_small], [b_small])
                act(rv[0:rowsT, 0:1], mv[0:rowsT, 1:2], AF.Sqrt, [b_small], [b_small], bias=EPS_AP[0:rowsT, 0:1], scale=1.0)
                recip(rv[0:rowsT, 1:2], rv[0:rowsT, 0:1], [b_small], [b_small])
                ts(vtok[0:rowsT, tb, :], psum[0:rowsT, b, :], mv[0:rowsT, 0:1], rv[0:rowsT, 1:2], ALU.subtract, ALU.mult,
                   [bps[b], b_small], [bRD[tb]])
                tt(vtok[0:rowsT, tb, :], vtok[0:rowsT, tb, :], Gbb[0:rowsT, :], ALU.mult, [bRD[tb]] + RG0, [bRD[tb]])
                tt(vtok[0:rowsT, tb, :], vtok[0:rowsT, tb, :], Bbb[0:rowsT, :], ALU.add, [bRD[tb]] + RG1, [bRD[tb]])
                cp_(vtb[0:rowsT, tb, :], vtok[0:rowsT, tb, :], [bRD[tb]], [bRF[tb]], eng="act")
                if sample:
                    S.dma("sp", O["sgv"][l], vtok[0:rowsT, 0, :], [bRD[0]], ())

            proj_tm(wl, 3584, T, cons_s, wkey=("win", l))
            wvu = wl.rearrange("(c p) m -> p c m", p=128)
            uslot = None
            for g in range(4):
                bsb = ps()
                for n_ in range(nb):
                    mm(psum[:, bsb, n_ * 128:n_ * 128 + rowsT], vtb[0:rowsT, n_, g * 128:(g + 1) * 128],
                       WT[0:rowsT, l, g, 0:rowsT], True, False, [bRF[n_], b_par], [bps[bsb]], skip_group_check=True)
                    mm(psum[:, bsb, n_ * 128:n_ * 128 + rowsT], ones_r[0:1, :],
                       b2r[0:1, g * 128:g * 128 + rowsT], False, True,
                       [b_const, b_b2], [bps[bsb]], skip_group_check=True)
                cp_(tsb[:, 0:T], psum[:, bsb, 0:T], [bps[bsb]], RG2, eng="act")
                if g % 2 == 0:
                    uslot, ubs = wslab(wvu[:, :, 3072 + (g // 2) * 256:3072 + (g // 2 + 1) * 256], [16, 256], key=(("win", l), 3072 + (g // 2) * 256))
                b = ps()
                for k in range(16):
                    mm(psum[:, b, 0:T], uslot[:, k, (g % 2) * 128:(g % 2 + 1) * 128], hT[:, k, 0:T], k == 0, k == 15,
                       [ubs, bh[k]], [bps[b]])
                tt(mixT[:, 12 + g, 0:T], psum[:, b, 0:T], tsb[:, 0:T], ALU.mult, [bps[b]] + RG2, [bmix[12 + g]])

        if not full or STOP <= 5:
            return
        wv = w_out[l].rearrange("(c p) m -> p c m", p=128)
        for sj in range(8):
            slot, bs = wslab(wv[:, :, sj * 256:(sj + 1) * 256], [16, 256], key=("wout", l, sj))
            for mc in range(2):
                m = sj * 2 + mc
                b = ps()
                for k in range(16):
                    mm(psum[:, b, 0:T], slot[:, k, mc * 128:(mc + 1) * 128], mixT[:, k, 0:T], k == 0, k == 15,
                       [bs, bmix[k]], [bps[b]])
                stt(xT[:, m, 0:T], psum[:, b, 0:T], g1[:, m:m + 1], xT[:, m, 0:T], ALU.mult, ALU.add,
                    [bps[b], bx[m], b_par], [bx[m]])
        if STOP <= 6:
            return
        norm_to_h(T, a2, b2_)
        gv = wg[l].rearrange("(c p) m -> p c m", p=128)
        uv = wu[l].rearrange("(c p) m -> p c m", p=128)
        dv = wd[l].rearrange("(j p) m -> p j m", p=128)
        NG = NFF // 4

        def gu(gi):
            a_ = gi % 2
            for half in range(2):
                gs, gbs = wslab(gv[:, :, gi * 512 + half * 256:gi * 512 + (half + 1) * 256], [16, 256], key=("wg", l, gi, half))
                us, ubs_ = wslab(uv[:, :, gi * 512 + half * 256:gi * 512 + (half + 1) * 256], [16, 256], key=("wu", l, gi, half))
                for mc in range(2):
                    j = half * 2 + mc
                    bg = ps()
                    for k in range(16):
                        mm(psum[:, bg, 0:T], gs[:, k, mc * 128:(mc + 1) * 128], hT[:, k, 0:T], k == 0, k == 15,
                           [gbs, bh[k]], [bps[bg]])
                    bu = ps()
                    for k in range(16):
                        mm(psum[:, bu, 0:T], us[:, k, mc * 128:(mc + 1) * 128], hT[:, k, 0:T], k == 0, k == 15,
                           [ubs_, bh[k]], [bps[bu]])
                    act(sl[:, j % 2, 0:T], psum[:, bg, 0:T], AF.Silu, [bps[bg]], [bsl[j % 2]])
                    tt(hid[:, a_, j, 0:T], psum[:, bu, 0:T], sl[:, j % 2, 0:T], ALU.mult, [bps[bu], bsl[j % 2]],
                       [bRE[a_ * 4 + j]])

        def down(gi):
            a_ = gi % 2
            d0, db0 = wslab(dv[:, gi * 4:gi * 4 + 2, :], [2, 2048], key=("wd", l, gi, 0))
            d1, db1 = wslab(dv[:, gi * 4 + 2:gi * 4 + 4, :], [2, 2048], key=("wd", l, gi, 1))
            for m in range(16):
                b = ps()
                for j in range(4):
                    ds_, dbs = (d0, db0) if j < 2 else (d1, db1)
                    mm(psum[:, b, 0:T], ds_[:, j % 2, m * 128:(m + 1) * 128], hid[:, a_, j, 0:T], j == 0, j == 3,
                       [dbs, bRE[a_ * 4 + j]], [bps[b]])
                stt(xT[:, m, 0:T], psum[:, b, 0:T], g2[:, m:m + 1], xT[:, m, 0:T], ALU.mult, ALU.add,
                    [bps[b], bx[m], b_par], [bx[m]])

        gu(0)
        for gi in range(1, NG):
            gu(gi)
            down(gi - 1)
        down(NG - 1)

    def final_out(T, dst_rows):
        rms_stats(T)
        nb = max(T // 128, 1)
        rows = min(T, 128)
        for c in range(16):
            stt(xT[:, c, 0:T], xT[:, c, 0:T], fgT[:, c:c + 1], rstd[:, 0:T], ALU.mult, ALU.mult,
                [bx[c], brstd, b_par], [bx[c]])
        for tb in range(nb):
            for cg in range(4):
                b = ps()
                for j in range(4):
                    c = cg * 4 + j
                    tr(psum[0:rows, b, j * 128:(j + 1) * 128], xT[:, c, tb * 128:tb * 128 + rows], ident_f[:],
                       [bx[c], b_const], [bps[b]])
                cp_(xtok[0:rows, cg * 512:(cg + 1) * 512], psum[0:rows, b, :], [bps[b]], [bRA], eng=evac_eng())
            S.dma("sp", dst_rows[tb * 128:tb * 128 + rows, :], xtok[0:rows, :], [bRA], ())

    EPS_T = sb("eps_t", [128, 1], F32)
    EPS_AP = EPS_T
    memset(EPS_T[:], EPS, [b_const])

    Op = {"conv": convp, "k": kp, "v": vp, "pool": poolp}
    Os = {"conv": convs, "k": ks, "v": vs, "pool": pools, "sgv": sgv}
    for ti in range(NT):
        load_x(xp[ti * TP:(ti + 1) * TP, :], TP)
        halo = ti < NH
        for l in range(NL):
            if halo:
                if l > ti:
                    continue
                mode = "full" if l < ti else "kv"
            else:
                mode = "full"
            layer(l, TP, 0, mode, first=(ti == 0), halo_save=halo, first_real=(ti == NH and NH > 0),
                  want_state=(ti == NT - 1), O=Op)
        if not halo:
            final_out(TP, yp[(ti - NH) * TP:(ti - NH + 1) * TP, :])
    if do_sample:
        load_x(xs, TS)
        for l in range(NL):
            layer(l, TS, 1, "full", first=False, halo_save=False, first_real=False, want_state=True, O=Os)
        final_out(TS, ys)

    S.emit(es)
    es.close()
    return nc


_CACHE = {}


def kernel(x_prompt, x_sample, c_prompt, c_sample, cache_conv, cache_k, cache_v, cache_pool,
           ada_w, ada_b, norm1_g, norm2_g, w_in, conv_w, conv_b, conv_gn_g, conv_gn_b, conv_pw,
           rel_bias, pool_w, pool_scale, sg_ln_g, sg_ln_b, sg_w, sg_b, w_out,
           ffn_gate, ffn_up, ffn_down, final_g):
    dbg = os.environ.get("KDEBUG")
    if dbg:
        NH, NR, NL, smp = [int(v) for v in dbg.split(",")]
    else:
        NH, NR, NL, smp = 4, 8, 4, 1
    key = (NH, NR, NL, smp)
    if key not in _CACHE:
        _CACHE[key] = build_program(NH, NR, NL, bool(smp))
    nc = _CACHE[key]
    f = lambda a: np.ascontiguousarray(np.asarray(a, dtype=np.float32))
    fl = lambda a: np.ascontiguousarray(np.asarray(a, dtype=np.float32)[:NL])
    NT = NH + NR
    shared = {
        "ada_w": fl(ada_w), "ada_b": fl(ada_b).reshape(NL, 96, 128), "n1": fl(norm1_g).reshape(NL, 16, 128),
        "n2": fl(norm2_g).reshape(NL, 16, 128), "w_in": fl(w_in), "conv_w": fl(conv_w),
        "conv_b": fl(conv_b).reshape(NL, 4, 128), "gn_g": fl(conv_gn_g).reshape(NL, 4, 128),
        "gn_b": fl(conv_gn_b).reshape(NL, 4, 128), "conv_pw": fl(conv_pw), "rel_bias": fl(rel_bias),
        "pool_w": fl(pool_w), "pool_scale": fl(pool_scale).reshape(NL, 4, 128), "sg_g": fl(sg_ln_g),
        "sg_bn": fl(sg_ln_b), "sg_w": fl(sg_w), "sg_b": fl(sg_b).reshape(NL, 1, 512), "w_out": fl(w_out),
        "wg": fl(ffn_gate), "wu": fl(ffn_up), "wd": fl(ffn_down), "final_g": f(final_g).reshape(16, 128),
    }
    xpr = f(x_prompt)
    xsm = f(x_sample)
    in_maps = []
    for i in range(8):
        b, half = i // 2, i % 2
        start = half * 4096 - NH * TP
        xin = np.zeros((NT * TP, D), np.float32)
        lo = max(start, 0)
        hi = min(start + NT * TP, 8192)
        xin[lo - start:hi - start] = xpr[b, lo:hi]
        m = dict(shared)
        m.update({
            "xp": xin, "flag": np.full((128, 1), float(half), np.float32), "xs": xsm[i],
            "cp": f(c_prompt)[b].reshape(16, 128), "cs": f(c_sample)[i].reshape(16, 128),
            "cconv": f(cache_conv)[:NL, i], "ck": f(cache_k)[:NL, i].reshape(NL, 512, 512),
            "cv": f(cache_v)[:NL, i].reshape(NL, 512, 512), "cpool": f(cache_pool)[:NL, i],
        })
        in_maps.append({k: np.ascontiguousarray(v) for k, v in m.items()})
    res = run_bass_kernel_spmd(nc, in_maps, core_ids=list(range(8)))
    R = res.results
    B = 4
    y_prompt = np.zeros((B, 8192, D), np.float32)
    for i in range(8):
        b, half = i // 2, i % 2
        n = min(NR * TP, 4096)
        y_prompt[b, half * 4096:half * 4096 + n] = R[i]["yp"][:n]
    y_sample = np.stack([R[i]["ys"] for i in range(8)])
    odd = [2 * b + 1 for b in range(B)]
    conv_p = np.stack([R[i]["convp"] for i in odd], axis=1)
    conv_s = np.stack([R[i]["convs"] for i in range(8)], axis=1)
    k_p = np.stack([R[i]["kp"] for i in odd], axis=1).reshape(L, B, 512, 8, 64)
    v_p = np.stack([R[i]["vp"] for i in odd], axis=1).reshape(L, B, 512, 8, 64)
    k_s = np.stack([R[i]["ks"] for i in range(8)], axis=1).reshape(L, 8, TS, 8, 64)
    v_s = np.stack([R[i]["vs"] for i in range(8)], axis=1).reshape(L, 8, TS, 8, 64)
    pool_p = np.stack([R[i]["poolp"] for i in odd], axis=1)
    pool_s = np.stack([R[i]["pools"] for i in range(8)], axis=1)
    sgv_s = np.stack([R[i]["sgv"] for i in range(8)], axis=1)
    return (y_prompt, y_sample, conv_p, conv_s, k_p, v_p, k_s, v_s, pool_p, pool_s, sgv_s)
```
